# Optimizing a Trainium2 kernel written in Bass

```python
import jax, jax.numpy as jnp
from jax import lax
import numpy as np

D_MODEL = 1024
BATCH = 4
SEQ = 8192
DEPTH = 4

N_META = 16
D_CONV = D_MODEL
CONV_WIDTH = 31
D_RWKV = D_MODEL
HEAD_SIZE = 64
N_RWKV_HEADS = D_RWKV // HEAD_SIZE
RANK_W = 64
RANK_A = 64
RANK_G = 128
D_FF = 2816
LN_EPS = 1e-5
GN_EPS = 64e-5
ALPHA = (2 * DEPTH) ** 0.25
BETA = (8 * DEPTH) ** -0.25
RWKV_COLS = 3 * D_RWKV + RANK_W + RANK_A + RANK_G
N_IN = 2 * D_CONV + RWKV_COLS + 2 * D_MODEL

kernel_name = "hybrid_conformer_rwkv7_macaron_deepnorm"


def _layer_norm(x, g, b, eps=LN_EPS):
    xf = x.astype(jnp.float32)
    mu = jnp.mean(xf, axis=-1, keepdims=True)
    var = jnp.mean(jnp.square(xf - mu), axis=-1, keepdims=True)
    y = (xf - mu) * lax.rsqrt(var + eps)
    return (y * g.astype(jnp.float32) + b.astype(jnp.float32)).astype(x.dtype)


def _swiglu(x, wg, wu, wd):
    return (jax.nn.silu(x @ wg) * (x @ wu)) @ wd


def _shift(z):
    return jnp.pad(z, ((0, 0), (1, 0), (0, 0)))[:, :-1]


def _conv_branch(glu_a, glu_b, conv_w, conv_b, ln_g, ln_b, wo):
    u = glu_a * jax.nn.sigmoid(glu_b)
    kern = conv_w[:, None, :].astype(u.dtype)
    u = lax.conv_general_dilated(u, kern, window_strides=(1,),
                                 padding=[(CONV_WIDTH - 1, 0)],
                                 dimension_numbers=("NWC", "WIO", "NWC"),
                                 feature_group_count=D_CONV) + conv_b
    u = jax.nn.silu(_layer_norm(u, ln_g, ln_b))
    return u @ wo


def _rwkv7_scan(r, decay, k, v, kk, a):
    B, L, H, N = r.shape
    xs = tuple(jnp.moveaxis(t, 1, 0) for t in (r, decay, k, v, kk, a))

    def step(S, inp):
        r_t, w_t, k_t, v_t, kk_t, a_t = inp
        sa = jnp.einsum("bhvk,bhk->bhv", S, -kk_t)
        S = (S * w_t[:, :, None, :]
             + sa[..., None] * (kk_t * a_t)[:, :, None, :]
             + v_t[..., None] * k_t[:, :, None, :])
        y = jnp.einsum("bhvk,bhk->bhv", S, r_t)
        return S, y

    S0 = jnp.zeros((B, H, N, N), jnp.float32)
    _, ys = lax.scan(step, S0, xs)
    return jnp.moveaxis(ys, 0, 1)


def _rwkv_branch(zr, time_mix, w0, w_up, a0, a_up, g_up, k_k, k_a, r_k, lnx_g, lnx_b, wo):
    B, L, _ = zr.shape
    H, N = N_RWKV_HEADS, HEAD_SIZE
    zr = zr + (_shift(zr) - zr) * time_mix
    r, k, v, zw, za, zg = jnp.split(
        zr, np.cumsum([D_RWKV, D_RWKV, D_RWKV, RANK_W, RANK_A]).tolist(), axis=-1)
    w = -jax.nn.softplus(-(w0 + jnp.tanh(zw) @ w_up)) - 0.5
    decay = jnp.exp(-jnp.exp(w.astype(jnp.float32)))
    a = jax.nn.sigmoid(a0 + za @ a_up)
    g = jax.nn.sigmoid(zg) @ g_up
    kk = (k * k_k).reshape(B, L, H, N).astype(jnp.float32)
    kk = kk / jnp.maximum(jnp.linalg.norm(kk, axis=-1, keepdims=True), 1e-12)
    k = k * (1.0 + (a - 1.0) * k_a)
    hd = lambda t: t.reshape(B, L, H, N).astype(jnp.float32)
    rh, kh, vh, ah = hd(r), hd(k), hd(v), hd(a)
    y = _rwkv7_scan(rh, decay.reshape(B, L, H, N), kh, vh, kk, ah)
    mu = jnp.mean(y, axis=-1, keepdims=True)
    var = jnp.mean(jnp.square(y - mu), axis=-1, keepdims=True)
    y = ((y - mu) * lax.rsqrt(var + GN_EPS)).reshape(B, L, D_RWKV)
    y = y * lnx_g.astype(jnp.float32) + lnx_b.astype(jnp.float32)
    bonus = jnp.sum(rh * kh * r_k.astype(jnp.float32), axis=-1, keepdims=True) * vh
    y = (y + bonus.reshape(B, L, D_RWKV)).astype(zr.dtype) * g
    return y @ wo


def _mixer(x, w_in, time_mix, conv_w, conv_b, conv_ln_g, conv_ln_b, conv_wo,
           w0, w_up, a0, a_up, g_up, k_k, k_a, r_k, lnx_g, lnx_b, rwkv_wo, w_out):
    z = x @ w_in
    glu_a, glu_b, zr, gate_c, gate_r = jnp.split(
        z, np.cumsum([D_CONV, D_CONV, RWKV_COLS, D_MODEL]).tolist(), axis=-1)
    yc = _conv_branch(glu_a, glu_b, conv_w, conv_b, conv_ln_g, conv_ln_b, conv_wo)
    yr = _rwkv_branch(zr, time_mix, w0, w_up, a0, a_up, g_up, k_k, k_a, r_k,
                      lnx_g, lnx_b, rwkv_wo)
    h = jax.nn.sigmoid(gate_c) * yc + jax.nn.sigmoid(gate_r) * yr
    return h @ w_out


def setup_inputs(seed: int = 0) -> dict:
    key = jax.random.key(seed)
    ks = iter(jax.random.split(key, 40))
    nrm = lambda shape, s: jax.random.normal(next(ks), shape, jnp.float32) * s
    gain = lambda shape: 1.0 + nrm(shape, 0.02)
    D, F, Ld = D_MODEL, D_FF, DEPTH
    return {
        "x": nrm((BATCH, SEQ, D), 1.0),
        "meta_tokens": nrm((N_META, D), 1.0),
        "ln1_g": gain((Ld, D)), "ln1_b": nrm((Ld, D), 0.02),
        "ffn1_wg": nrm((Ld, D, F), D ** -0.5),
        "ffn1_wu": nrm((Ld, D, F), D ** -0.5),
        "ffn1_wd": nrm((Ld, F, D), F ** -0.5 * BETA),
        "w_in": nrm((Ld, D, N_IN), D ** -0.5),
        "time_mix": jax.random.uniform(next(ks), (Ld, RWKV_COLS), jnp.float32),
        "conv_w": nrm((Ld, CONV_WIDTH, D_CONV), CONV_WIDTH ** -0.5),
        "conv_b": nrm((Ld, D_CONV), 0.02),
        "conv_ln_g": gain((Ld, D_CONV)), "conv_ln_b": nrm((Ld, D_CONV), 0.02),
        "conv_wo": nrm((Ld, D_CONV, D), D_CONV ** -0.5),
        "w0": jax.random.uniform(next(ks), (Ld, D_RWKV), jnp.float32, -6.0, 1.0),
        "w_up": nrm((Ld, RANK_W, D_RWKV), 0.5 * RANK_W ** -0.5),
        "a0": nrm((Ld, D_RWKV), 0.1),
        "a_up": nrm((Ld, RANK_A, D_RWKV), RANK_A ** -0.5),
        "g_up": nrm((Ld, RANK_G, D_RWKV), RANK_G ** -0.5),
        "k_k": 0.85 + nrm((Ld, D_RWKV), 0.02),
        "k_a": gain((Ld, D_RWKV)),
        "r_k": nrm((Ld, N_RWKV_HEADS, HEAD_SIZE), 0.1),
        "lnx_g": gain((Ld, D_RWKV)), "lnx_b": nrm((Ld, D_RWKV), 0.02),
        "rwkv_wo": nrm((Ld, D_RWKV, D), D_RWKV ** -0.5),
        "w_out": nrm((Ld, D, D), D ** -0.5 * BETA),
        "ln2_g": gain((Ld, D)), "ln2_b": nrm((Ld, D), 0.02),
        "ffn2_wg": nrm((Ld, D, F), D ** -0.5),
        "ffn2_wu": nrm((Ld, D, F), D ** -0.5),
        "ffn2_wd": nrm((Ld, F, D), F ** -0.5 * BETA),
        "ln3_g": gain((Ld, D)), "ln3_b": nrm((Ld, D), 0.02),
    }


def reference(x, meta_tokens, ln1_g, ln1_b, ffn1_wg, ffn1_wu, ffn1_wd, w_in, time_mix,
              conv_w, conv_b, conv_ln_g, conv_ln_b, conv_wo, w0, w_up, a0, a_up, g_up,
              k_k, k_a, r_k, lnx_g, lnx_b, rwkv_wo, w_out, ln2_g, ln2_b,
              ffn2_wg, ffn2_wu, ffn2_wd, ln3_g, ln3_b):
    B = x.shape[0]
    meta = jnp.broadcast_to(meta_tokens.astype(x.dtype)[None], (B, N_META, D_MODEL))
    h = jnp.concatenate([meta, x], axis=1)
    for l in range(DEPTH):
        h = _layer_norm(ALPHA * h + 0.5 * _swiglu(h, ffn1_wg[l], ffn1_wu[l], ffn1_wd[l]),
                        ln1_g[l], ln1_b[l])
        m = _mixer(h, w_in[l], time_mix[l], conv_w[l], conv_b[l], conv_ln_g[l],
                   conv_ln_b[l], conv_wo[l], w0[l], w_up[l], a0[l], a_up[l], g_up[l],
                   k_k[l], k_a[l], r_k[l], lnx_g[l], lnx_b[l], rwkv_wo[l], w_out[l])
        h = _layer_norm(ALPHA * h + m, ln2_g[l], ln2_b[l])
        h = _layer_norm(ALPHA * h + 0.5 * _swiglu(h, ffn2_wg[l], ffn2_wu[l], ffn2_wd[l]),
                        ln3_g[l], ln3_b[l])
    return h[:, N_META:]
```

```python
import numpy as np
from contextlib import ExitStack
import concourse.bass as bass
import concourse.mybir as mybir
from concourse.bass_utils import run_bass_kernel_spmd

F32 = mybir.dt.float32
BF16 = mybir.dt.bfloat16
AF = mybir.ActivationFunctionType
ALU = mybir.AluOpType

D = 1024
NB = 4
SEQ = 8192
NMETA = 16
T = SEQ + NMETA
DEPTH = 4
DFF = 2816
CW = 31
NH = 16
HS = 64
RW, RA, RG = 64, 64, 128
RWKV_COLS = 3 * D + RW + RA + RG
NIN = 2 * D + RWKV_COLS + 2 * D
LN_EPS = 1e-5
GN_EPS = 64e-5
ALPHA = (2 * DEPTH) ** 0.25
C0 = float(np.exp(-0.5))
CH = 64
TPB = 8256

SAME_ENGINE_SYNC = True
HM_ENG = 'gpsimd'
C_STAGE = 3


class Buf:
    __slots__ = ("name", "w", "r", "const")

    def __init__(self, name, const=False):
        self.name = name
        self.w = None
        self.r = {}
        self.const = const


class Sched:
    ENGS = ("sync", "gpsimd", "tensor", "scalar", "vector")

    def __init__(self, nc, stack):
        self.nc = nc
        self.stack = stack
        self.streams = {e: [] for e in self.ENGS}
        self.esem = {e: stack.enter_context(nc.semaphore("es_" + e)) for e in self.ENGS}
        self.cnt = {e: 0 for e in self.ENGS}
        self.seen = {e: {} for e in self.ENGS}
        self.dsem = {}
        self.dcnt = {}
        self.semobj = {}
        for e in self.ENGS:
            self.semobj[id(self.esem[e])] = self.esem[e]

    def _wait(self, eng, ev):
        if ev is None:
            return
        sem, val = ev
        own = sem is self.esem[eng]
        if own and (eng == "tensor" or eng == "sync" or not SAME_ENGINE_SYNC):
            return
        k = id(sem)
        if self.seen[eng].get(k, 0) >= val:
            return
        self.seen[eng][k] = val
        self.streams[eng].append(("w", sem, val))

    defer = None

    def flush(self, thunks):
        for t in thunks:
            self.op(*t)

    def op(self, eng, fn, reads=(), writes=(), dma=None):
        if self.defer is not None:
            self.defer.append((eng, fn, list(reads), list(writes), dma))
            return None
        deps = []
        for b in reads:
            deps.append(b.w)
        for b in writes:
            deps.append(b.w)
            deps.extend(b.r.values())
        for ev in deps:
            self._wait(eng, ev)
        if dma is not None:
            if dma not in self.dsem:
                self.dsem[dma] = self.stack.enter_context(self.nc.semaphore("ds_%s" % dma))
                self.dcnt[dma] = 0
            self.dcnt[dma] += 16
            ev = (self.dsem[dma], self.dcnt[dma])
            inc = 16
        else:
            self.cnt[eng] += 1
            ev = (self.esem[eng], self.cnt[eng])
            inc = 1
        self.streams[eng].append(("o", fn, ev[0], inc))
        for b in reads:
            if not b.const:
                k = id(ev[0])
                b.r[k] = ev
        for b in writes:
            b.w = ev
            b.r = {}
        return ev

    def barrier(self):
        for e in self.ENGS:
            for x in self.ENGS:
                if x != e and self.cnt[x] > 0:
                    self._wait(e, (self.esem[x], self.cnt[x]))
            for k in self.dsem:
                self._wait(e, (self.dsem[k], self.dcnt[k]))

    def final_wait(self, eng, bufs):
        for b in bufs:
            self._wait(eng, b.w)

    def emit(self):
        nc = self.nc
        streams = self.streams

        def replay(e, st):
            pend = None
            for it in st:
                if it[0] == "w":
                    if pend is not None:
                        e.wait_ge(pend[1], pend[2])
                    pend = it
                else:
                    ins = it[1](e)
                    if pend is not None:
                        ins = ins._wait_ge(pend[1], pend[2])
                        pend = None
                    ins.then_inc(it[2], it[3])
            if pend is not None:
                e.wait_ge(pend[1], pend[2])

        with nc.Block() as block:
            @block.sync
            def _(e):
                replay(e, streams["sync"])

            @block.gpsimd
            def _(e):
                replay(e, streams["gpsimd"])

            @block.tensor
            def _(e):
                replay(e, streams["tensor"])

            @block.scalar
            def _(e):
                replay(e, streams["scalar"])

            @block.vector
            def _(e):
                replay(e, streams["vector"])


class Ctx:
    UID = [0]

    def __init__(self, nc, stack, S=None):
        self.nc = nc
        self.stack = stack
        self.S = S if S is not None else Sched(nc, stack)
        self.n = 0

    def sb(self, shape, dt=F32, name=None, const=False):
        self.n += 1
        Ctx.UID[0] += 1
        name = "s%d_" % Ctx.UID[0] + (name or "t%d" % self.n)
        t = self.stack.enter_context(self.nc.sbuf_tensor(name, list(shape), dt))
        return t, Buf(name, const)

    def ps(self, shape, dt=F32, name=None):
        self.n += 1
        Ctx.UID[0] += 1
        name = "p%d_" % Ctx.UID[0] + (name or "p%d" % self.n)
        t = self.stack.enter_context(self.nc.psum_tensor(name, list(shape), dt))
        return t, Buf(name)

    def dram(self, name, shape, dt=F32, kind="ExternalInput"):
        return self.nc.dram_tensor(name, list(shape), dt, kind=kind).ap(), Buf(name)


def build_B(nchunks):
    TP1 = nchunks * CH + 1
    nc = bass.Bass("TRN2", target_bir_lowering=False)
    stack = ExitStack()
    with stack:
        cx = Ctx(nc, stack)
        H = 8
        dr = lambda name, shape, kind="ExternalInput": nc.dram_tensor(name, list(shape), F32, kind=kind).ap()
        d = {"rkv_v": dr("rkv", [3, H, 64, TP1]).rearrange("w h c t -> c w h t"),
             "zw": dr("zw", [64, TP1]), "za": dr("za", [64, TP1]), "zg": dr("zg", [128, TP1]),
             "pv": dr("pv", [64, 11, H]), "psm": dr("psm", [128, 4]), "wup": dr("wup", [65, H, 64]),
             "aup": dr("aup", [65, H, 64]), "gup": dr("gup", [128, H, 64]), "cst": dr("cst", [64, 6, 64]),
             "yo_v": dr("yo", [H, 64, nchunks * CH], "ExternalOutput").rearrange("h c t -> c h t")}
        PB = [cx.ps([128, 512], name="pb%d" % i) for i in range(8)]
        emit_B(cx, PB, d, nchunks)
        S = cx.S
        for k in S.dsem:
            S._wait("sync", (S.dsem[k], S.dcnt[k]))
        S.emit()
    return nc


def emit_B(cx, PB, d, nchunks):
    if True:
        S = cx.S
        op = S.op
        H = 8
        rkv_v, zw, za, zg, pv, ps_, wup, aup, gup, cst, yo_v = (d[k] for k in (
            "rkv_v", "zw", "za", "zg", "pv", "psm", "wup", "aup", "gup", "cst", "yo_v"))
        rkv_b = zw_b = za_b = zg_b = None

        pv_t, pv_tb = cx.sb([64, 11, H], name="pv_t", const=True)
        psm_t, psm_tb = cx.sb([128, 4], name="psm_t", const=True)
        wup_t, wup_tb = cx.sb([65, H, 64], name="wup_t", const=True)
        aup_t, aup_tb = cx.sb([65, H, 64], name="aup_t", const=True)
        gup_t, gup_tb = cx.sb([128, H, 64], name="gup_t", const=True)
        cst_t, cst_tb = cx.sb([64, 6, 64], name="cst_t", const=True)
        rst_t, rst_tb = cx.sb([64, H, 64], name="rst_t", const=True)
        for (dst, dstb, src, key) in ((pv_t, pv_tb, pv, "c0"), (psm_t, psm_tb, ps_, "c1"), (wup_t, wup_tb, wup, "c2"),
                                      (aup_t, aup_tb, aup, "c3"), (gup_t, gup_tb, gup, "c4"), (cst_t, cst_tb, cst, "c5")):
            op("sync", (lambda e, d=dst, s=src: e.dma_start(out=d[:], in_=s)), writes=[dstb], dma=key)
        op("vector", lambda e: e.memset(rst_t[:], 1.0), writes=[rst_tb])
        op("vector", lambda e: e.memset(rst_t[:, :, 0:1], 0.0), writes=[rst_tb])

        def bc_h(ap2d):
            return ap2d.unsqueeze(1).to_broadcast([64, H, 64])

        def bc_t(ap2d):
            return ap2d.unsqueeze(2).to_broadcast([64, H, 64])

        m_su, m_u, m_sl, ident, ones = (cst_t[:, i, :] for i in range(5))
        m_uu = cst_t[:, 0:2, :]
        msw_t, msw_tb = cx.sb([64, 2, 64], name="msw_t", const=True)
        op("vector", lambda e: e.tensor_copy(out=msw_t[:, 0, :], in_=cst_t[:, 1, :]), reads=[cst_tb], writes=[msw_tb])
        op("vector", lambda e: e.tensor_copy(out=msw_t[:, 1, :], in_=cst_t[:, 0, :]), reads=[cst_tb], writes=[msw_tb])
        onesm_t, onesm_tb = cx.sb([64, 64], name="onesm_t", const=True)
        op("vector", lambda e: e.tensor_scalar(out=onesm_t[:], in0=cst_t[:, 4, :], scalar1=1.0 / 64, scalar2=None, op0=ALU.mult),
           reads=[cst_tb], writes=[onesm_tb])

        def sbn(shape, name):
            return cx.sb(shape, name=name)

        NBUF = 2
        Zin = [sbn([64, 3, H, CH + 1], "Zin%d" % i) for i in range(NBUF)]
        ZWi = [sbn([64, CH + 1], "ZWi%d" % i) for i in range(NBUF)]
        ZAi = [sbn([64, CH + 1], "ZAi%d" % i) for i in range(NBUF)]
        ZGi = [sbn([128, CH + 1], "ZGi%d" % i) for i in range(NBUF)]
        Dt = sbn([64, 3, H, CH], "Dt")
        XSs = [sbn([64, 3, H, CH], "XS%d" % i) for i in range(2)]
        Dw = sbn([64, CH], "Dw"); Da = sbn([64, CH], "Da"); Dg = sbn([128, CH], "Dg")
        TW = sbn([65, CH], "TW"); ZAs = sbn([65, CH], "ZAs"); SG = sbn([128, CH], "SG")
        SIGW = sbn([64, H, CH], "SIGW"); CUM = sbn([64, H, CH], "CUM"); CUMP = sbn([64, H, CH], "CUMP")
        Ps = [sbn([64, H, CH], "P%d" % i) for i in range(2)]; IP = sbn([64, H, CH], "IP"); PP = sbn([64, H, CH], "PP")
        A_ = sbn([64, H, CH], "A_"); KK = sbn([64, H, CH], "KK"); SQ = sbn([64, H, CH], "SQ")
        RN = sbn([64, H, CH], "RN"); T1 = sbn([64, H, CH], "T1"); KMs = [sbn([64, H, CH], "KM%d" % i) for i in range(2)]
        B0 = sbn([64, H, CH], "B0")
        RAs = [sbn([64, H, 2, CH], "RA%d" % i) for i in range(2)]
        BTs = [sbn([64, H, CH], "BT%d" % i) for i in range(2)]; KTs = [sbn([64, H, CH], "KT%d" % i) for i in range(2)]
        BHs = [sbn([64, H, CH], "BH%d" % i) for i in range(2)]; KHs = [sbn([64, H, CH], "KH%d" % i) for i in range(2)]
        G1S = sbn([64, H, 3, CH], "G1S")
        G2S = sbn([64, H, 2, CH], "G2S")
        XTa = sbn([64, H, CH], "XTa"); XTb = sbn([64, H, CH], "XTb")
        XMa = sbn([64, H, 2, CH], "XMa"); XMb = sbn([64, H, 2, CH], "XMb")
        VT = sbn([64, H, CH], "VT"); KHT = sbn([64, H, CH], "KHT"); BHT = sbn([64, H, CH], "BHT")
        WT = sbn([64, H, CH], "WT"); UT = sbn([64, H, CH], "UT")
        ST = [sbn([64, H, CH], "ST%d" % i) for i in range(2)]
        STP = sbn([64, H, CH], "STP")
        YS = sbn([64, H, CH], "YS"); YQ = sbn([64, H, CH], "YQ"); MEAN = sbn([64, H, CH], "MEAN")
        MSQ = sbn([64, H, CH], "MSQ"); VAR = sbn([64, H, CH], "VAR"); RK = sbn([64, H, CH], "RK")
        GSs = [sbn([64, H, CH], "GS%d" % i) for i in range(2)]
        YO = [sbn([64, H, CH], "YO%d" % i) for i in range(2)]
        def pbv(i):
            return PB[i][0][0:64, :].rearrange("p (h t) -> p h t", h=H)

        def pb2(i):
            raise NotImplementedError

        op("vector", lambda e: e.memset(ST[0][0][:], 0.0), writes=[ST[0][1]])
        op("vector", lambda e: e.memset(TW[0][64:65, :], 1.0), writes=[TW[1]])
        op("vector", lambda e: e.memset(ZAs[0][64:65, :], 1.0), writes=[ZAs[1]])


        def load(c):
            i = c % NBUF
            t0 = c * CH
            for w in range(3):
                op("sync", lambda e, w=w: e.dma_start(out=Zin[i][0][:, w], in_=rkv_v[:, w, :, t0:t0 + CH + 1]),
                   writes=[Zin[i][1]], dma="zin%d" % i)
            op("sync", lambda e: e.dma_start(out=ZWi[i][0][:], in_=zw[:, t0:t0 + CH + 1]),
               writes=[ZWi[i][1]], dma="zwi%d" % i)
            op("sync", lambda e: e.dma_start(out=ZAi[i][0][:], in_=za[:, t0:t0 + CH + 1]),
               writes=[ZAi[i][1]], dma="zai%d" % i)
            op("sync", lambda e: e.dma_start(out=ZGi[i][0][:], in_=zg[:, t0:t0 + CH + 1]),
               writes=[ZGi[i][1]], dma="zgi%d" % i)

        def TT(eng, out, in0, in1, alu, reads, writes):
            op(eng, lambda e: e.tensor_tensor(out=out, in0=in0, in1=in1, op=alu), reads=reads, writes=writes)

        def ACT(out, in_, func, reads, writes, scale=None, bias=None):
            kw = {}
            if scale is not None:
                kw["scale"] = scale
            if bias is not None:
                kw["bias"] = bias
            op("scalar", lambda e: e.activation(out=out, in_=in_, func=func, **kw), reads=reads, writes=writes)

        def MM(out, lhsT, rhs, reads, writes, start=True, stop=True):
            op("tensor", lambda e: e.matmul(out, lhsT, rhs, start=start, stop=stop), reads=reads, writes=writes)

        def prep(c):
            XS, KM, RA, BT, KT, BH, KH, P, GS = (t[c % 2] for t in (XSs, KMs, RAs, BTs, KTs, BHs, KHs, Ps, GSs))
            i = c % NBUF
            zin, zin_b = Zin[i]
            cur = zin[:, :, :, 1:CH + 1]
            prv = zin[:, :, :, 0:CH]
            TT("gpsimd", Dt[0][:], prv, cur, ALU.subtract, [zin_b], [Dt[1]])
            tm3 = pv_t[:, 0:3, :].unsqueeze(3).to_broadcast([64, 3, H, CH])
            TT("gpsimd", Dt[0][:], Dt[0][:], tm3, ALU.mult, [Dt[1], pv_tb], [Dt[1]])
            TT("vector", XS[0][:], Dt[0][:], cur, ALU.add, [Dt[1], zin_b], [XS[1]])
            xr, xk, xv = XS[0][:, 0], XS[0][:, 1], XS[0][:, 2]
            for (zi, dd, col, npart, dst) in ((ZWi[i], Dw, 0, 64, None), (ZAi[i], Da, 1, 64, ZAs), (ZGi[i], Dg, 2, 128, None)):
                TT("gpsimd", dd[0][:], zi[0][:, 0:CH], zi[0][:, 1:CH + 1], ALU.subtract, [zi[1]], [dd[1]])
                o = dst[0][0:64, :] if dst is not None else dd[0][:]
                ob = dst[1] if dst is not None else dd[1]
                op("vector", lambda e, o=o, dd=dd, col=col, npart=npart, zi=zi: e.scalar_tensor_tensor(
                    out=o, in0=dd[0][:], scalar=psm_t[0:npart, col:col + 1], in1=zi[0][:, 1:CH + 1],
                    op0=ALU.mult, op1=ALU.add), reads=[dd[1], zi[1], psm_tb], writes=[ob])
            ACT(TW[0][0:64, :], Dw[0][:], AF.Tanh, [Dw[1]], [TW[1]])
            ACT(SG[0][:], Dg[0][:], AF.Sigmoid, [Dg[1]], [SG[1]])
            for h in range(H):
                MM(pbv(3)[:, h, :], wup_t[:, h, :], TW[0][:], [wup_tb, TW[1]], [PB[3][1]])
            for h in range(H):
                MM(pbv(6)[:, h, :], aup_t[:, h, :], ZAs[0][:], [aup_tb, ZAs[1]], [PB[6][1]])
            for h in range(H):
                MM(pbv(7)[:, h, :], gup_t[:, h, :], SG[0][:], [gup_tb, SG[1]], [PB[7][1]])
            ACT(SIGW[0][:], pbv(3), AF.Sigmoid, [PB[3][1]], [SIGW[1]])
            ACT(A_[0][:], pbv(6), AF.Sigmoid, [PB[6][1]], [A_[1]])
            ACT(GS[0][:], pbv(7), AF.Copy, [PB[7][1]], [GS[1]])
            op("vector", lambda e: e.tensor_tensor_scan(
                out=CUM[0][:].rearrange("p h t -> p (h t)"), data0=rst_t[:].rearrange("p h t -> p (h t)"),
                data1=SIGW[0][:].rearrange("p h t -> p (h t)"), initial=0.0, op0=ALU.mult, op1=ALU.add),
               reads=[SIGW[1], rst_tb], writes=[CUM[1]])
            TT("gpsimd", CUMP[0][:], CUM[0][:], SIGW[0][:], ALU.subtract, [CUM[1], SIGW[1]], [CUMP[1]])
            ACT(P[0][:], CUM[0][:], AF.Exp, [CUM[1]], [P[1]], scale=-C0)
            ACT(IP[0][:], CUM[0][:], AF.Exp, [CUM[1]], [IP[1]], scale=C0)
            ACT(PP[0][:], CUMP[0][:], AF.Exp, [CUMP[1]], [PP[1]], scale=-C0)
            TT("gpsimd", KK[0][:], xk, bc_t(pv_t[:, 3, :]), ALU.mult, [XS[1], pv_tb], [KK[1]])
            TT("gpsimd", SQ[0][:], KK[0][:], KK[0][:], ALU.mult, [KK[1]], [SQ[1]])
            MM(PB[3][0][0:64, :], ones, SQ[0][:].rearrange("p h t -> p (h t)"), [cst_tb, SQ[1]], [PB[3][1]])
            op("vector", lambda e: e.tensor_scalar(out=RN[0][:], in0=pbv(3), scalar1=1e-24, scalar2=None, op0=ALU.max),
               reads=[PB[3][1]], writes=[RN[1]])
            ACT(RN[0][:], RN[0][:], AF.Sqrt, [RN[1]], [RN[1]])
            op("vector", lambda e: e.reciprocal(out=RN[0][:], in_=RN[0][:]), reads=[RN[1]], writes=[RN[1]])
            TT("vector", KK[0][:], KK[0][:], RN[0][:], ALU.mult, [KK[1], RN[1]], [KK[1]])
            op("vector", lambda e: e.scalar_tensor_tensor(out=T1[0][:], in0=A_[0][:], scalar=-1.0, in1=bc_t(pv_t[:, 4, :]),
                                                          op0=ALU.add, op1=ALU.mult), reads=[A_[1], pv_tb], writes=[T1[1]])
            op("vector", lambda e: e.scalar_tensor_tensor(out=KM[0][:], in0=T1[0][:], scalar=1.0, in1=xk,
                                                          op0=ALU.add, op1=ALU.mult), reads=[T1[1], XS[1]], writes=[KM[1]])
            TT("vector", B0[0][:], KK[0][:], A_[0][:], ALU.mult, [KK[1], A_[1]], [B0[1]])
            TT("vector", RA[0][:, :, 0, :], xr, P[0][:], ALU.mult, [XS[1], P[1]], [RA[1]])
            op("vector", lambda e: e.scalar_tensor_tensor(out=RA[0][:, :, 1, :], in0=KK[0][:], scalar=-1.0, in1=PP[0][:],
                                                          op0=ALU.mult, op1=ALU.mult), reads=[KK[1], PP[1]], writes=[RA[1]])
            TT("gpsimd", BT[0][:], B0[0][:], IP[0][:], ALU.mult, [B0[1], IP[1]], [BT[1]])
            TT("gpsimd", KT[0][:], KM[0][:], IP[0][:], ALU.mult, [KM[1], IP[1]], [KT[1]])
            pc_b = P[0][:, :, CH - 1:CH].to_broadcast([64, H, CH])
            TT("gpsimd", BH[0][:], BT[0][:], pc_b, ALU.mult, [BT[1], P[1]], [BH[1]])
            TT("gpsimd", KH[0][:], KT[0][:], pc_b, ALU.mult, [KT[1], P[1]], [KH[1]])

        def body(c, thunks):
            XS, KM, RA, BT, KT, BH, KH, P, GS = (t[c % 2] for t in (XSs, KMs, RAs, BTs, KTs, BHs, KHs, Ps, GSs))
            i = c % NBUF
            xr, xk, xv = XS[0][:, 0], XS[0][:, 1], XS[0][:, 2]
            pc_b = P[0][:, :, CH - 1:CH].to_broadcast([64, H, CH])
            def g2v(b0):
                return [PB[b0 + hh // 4][0][0:64, (hh % 4) * 128:(hh % 4) * 128 + 128] for hh in range(H)]
            g1 = g2v(4)
            g2 = g2v(6)
            for h in range(H):
                ra_h = RA[0][:, h, :, :].rearrange("p a t -> p (a t)")
                MM(g1[h], BT[0][:, h, :], ra_h, [BT[1], RA[1]], [PB[4 + h // 4][1]])
            for h in range(H):
                ra_h = RA[0][:, h, :, :].rearrange("p a t -> p (a t)")
                MM(g2[h], KT[0][:, h, :], ra_h, [KT[1], RA[1]], [PB[6 + h // 4][1]])
            for h in range(H):
                MM(pbv(0)[:, h, :], RA[0][:, h, 1, :], BT[0][:, h, :], [BT[1], RA[1]], [PB[0][1]])
            for half in range(2):
                hs = slice(half * 4, half * 4 + 4)
                src = PB[4 + half][0][0:64, :].rearrange("p (h a t) -> p h a t", h=4, a=2)
                msk = msw_t[:].unsqueeze(1).to_broadcast([64, 4, 2, CH])
                TT("vector", G1S[0][:, hs, 0:2, :], src, msk, ALU.mult, [PB[4 + half][1], msw_tb], [G1S[1]])
                src2 = PB[6 + half][0][0:64, :].rearrange("p (h a t) -> p h a t", h=4, a=2)
                TT("vector", G2S[0][:, hs, :, :], src2, msk, ALU.mult, [PB[6 + half][1], msw_tb], [G2S[1]])
            TT("vector", XTa[0][:], pbv(0), bc_h(m_sl), ALU.mult, [PB[0][1], cst_tb], [XTa[1]])
            TT("gpsimd", G1S[0][:, :, 2, :], G1S[0][:, :, 1, :], bc_h(ident), ALU.add, [G1S[1], cst_tb], [G1S[1]])
            Xp = (G1S, lambda h: G1S[0][:, h, 1, :], lambda h: G1S[0][:, h, 1:3, :].rearrange("p a t -> p (a t)"))
            XTp = XTa
            XTn = XTb
            XMn, XMo = XMa, XMb
            for h in range(H):
                MM(pbv(1)[:, h, :], XTp[0][:, h, :], Xp[1](h), [XTp[1], Xp[0][1]], [PB[1][1]])
            for h in range(H):
                MM(pbv(2)[:, h, :], Xp[1](h), XTp[0][:, h, :], [XTp[1], Xp[0][1]], [PB[2][1]])
            ACT(XMn[0][:, :, 0, :], pbv(1), AF.Copy, [PB[1][1]], [XMn[1]])
            op("gpsimd", lambda e, XMn=XMn: e.tensor_copy(out=XMn[0][:, :, 1, :], in_=G1S[0][:, :, 2, :]), reads=[G1S[1]], writes=[XMn[1]])
            ACT(XTn[0][:], pbv(2), AF.Copy, [PB[2][1]], [XTn[1]])
            XMc, XTc = XMn, XTn
            XMn = XMo
            XTn = XTa
            nth = len(thunks)
            per = (nth + 5) // 6
            S.flush(thunks[0:per])
            for lvl in range(2, 7):
                last = lvl == 6
                pa = g2v(4)
                for h in range(H):
                    if last:
                        MM(pa[h][:, 64:128], XTc[0][:, h, :], XMc[0][:, h, 1, :], [XTc[1], XMc[1]], [PB[4 + h // 4][1]])
                    else:
                        MM(pa[h], XTc[0][:, h, :], XMc[0][:, h, :, :].rearrange("p a t -> p (a t)"),
                           [XTc[1], XMc[1]], [PB[4 + h // 4][1]])
                if not last:
                    for h in range(H):
                        MM(pbv(0)[:, h, :], XMc[0][:, h, 0, :], XTc[0][:, h, :], [XTc[1], XMc[1]], [PB[0][1]])
                for half in range(2):
                    hs = slice(half * 4, half * 4 + 4)
                    src = PB[4 + half][0][0:64, :].rearrange("p (h a t) -> p h a t", h=4, a=2)
                    if not last:
                        ACT(XMn[0][:, hs, 0, :], src[:, :, 0, :], AF.Copy, [PB[4 + half][1]], [XMn[1]])
                    TT("vector", XMn[0][:, hs, 1, :], src[:, :, 1, :], XMc[0][:, hs, 1, :], ALU.add,
                       [PB[4 + half][1], XMc[1]], [XMn[1]])
                if not last:
                    ACT(XTn[0][:], pbv(0), AF.Copy, [PB[0][1]], [XTn[1]])
                XMc, XMn = XMn, XMc
                XTc, XTn = XTn, XTc
                S.flush(thunks[(lvl - 1) * per:lvl * per])
            Mfin = XMc
            for (src_ap, srcb, bank, dst) in ((lambda h: xv[:, h, :], XS[1], 1, VT), (lambda h: KH[0][:, h, :], KH[1], 2, KHT),
                                              (lambda h: BH[0][:, h, :], BH[1], 3, BHT)):
                for h in range(H):
                    op("tensor", lambda e, h=h, src_ap=src_ap, bank=bank: e.transpose(pbv(bank)[:, h, :], src_ap(h), ident),
                       reads=[srcb, cst_tb], writes=[PB[bank][1]])
                ACT(dst[0][:], pbv(bank), AF.Copy, [PB[bank][1]], [dst[1]])
            Sc, Sn = ST[c % 2], ST[(c + 1) % 2]
            TT("gpsimd", STP[0][:], Sc[0][:], pc_b, ALU.mult, [Sc[1], P[1]], [STP[1]])
            for h in range(H):
                MM(pbv(6)[:, h, :], G2S[0][:, h, 1, :], VT[0][:, h, :], [G2S[1], VT[1]], [PB[6][1]], start=True, stop=False)
                MM(pbv(6)[:, h, :], RA[0][:, h, 1, :], Sc[0][:, h, :], [RA[1], Sc[1]], [PB[6][1]], start=False, stop=True)
            ACT(WT[0][:], pbv(6), AF.Copy, [PB[6][1]], [WT[1]])
            for h in range(H):
                MM(pbv(7)[:, h, :], Mfin[0][:, h, 1, :], WT[0][:, h, :], [Mfin[1], WT[1]], [PB[7][1]])
            ACT(UT[0][:], pbv(7), AF.Copy, [PB[7][1]], [UT[1]])
            for h in range(H):
                MM(pbv(6)[:, h, :], BHT[0][:, h, :], UT[0][:, h, :], [BHT[1], UT[1]], [PB[6][1]], start=True, stop=False)
                MM(pbv(6)[:, h, :], KHT[0][:, h, :], VT[0][:, h, :], [KHT[1], VT[1]], [PB[6][1]], start=False, stop=True)
            for h in range(H):
                MM(pbv(7)[:, h, :], Sc[0][:, h, :], RA[0][:, h, 0, :], [Sc[1], RA[1]], [PB[7][1]], start=True, stop=False)
                MM(pbv(7)[:, h, :], UT[0][:, h, :], G1S[0][:, h, 0, :], [UT[1], G1S[1]], [PB[7][1]], start=False, stop=False)
                MM(pbv(7)[:, h, :], VT[0][:, h, :], G2S[0][:, h, 0, :], [VT[1], G2S[1]], [PB[7][1]], start=False, stop=True)
            TT("vector", Sn[0][:], STP[0][:], pbv(6), ALU.add, [STP[1], PB[6][1]], [Sn[1]])
            ACT(YS[0][:], pbv(7), AF.Copy, [PB[7][1]], [YS[1]])
            TT("gpsimd", YQ[0][:], YS[0][:], YS[0][:], ALU.mult, [YS[1]], [YQ[1]])
            TT("gpsimd", RK[0][:], xr, KM[0][:], ALU.mult, [XS[1], KM[1]], [RK[1]])
            TT("gpsimd", RK[0][:], RK[0][:], bc_t(pv_t[:, 6, :]), ALU.mult, [RK[1], pv_tb], [RK[1]])
            MM(PB[1][0][0:64, :], onesm_t[:], YS[0][:].rearrange("p h t -> p (h t)"), [onesm_tb, YS[1]], [PB[1][1]])
            MM(PB[2][0][0:64, :], onesm_t[:], YQ[0][:].rearrange("p h t -> p (h t)"), [onesm_tb, YQ[1]], [PB[2][1]])
            MM(PB[3][0][0:64, :], ones, RK[0][:].rearrange("p h t -> p (h t)"), [cst_tb, RK[1]], [PB[3][1]])
            ACT(MEAN[0][:], pbv(1), AF.Copy, [PB[1][1]], [MEAN[1]])
            TT("gpsimd", MSQ[0][:], MEAN[0][:], MEAN[0][:], ALU.mult, [MEAN[1]], [MSQ[1]])
            TT("vector", VAR[0][:], pbv(2), MSQ[0][:], ALU.subtract, [PB[2][1], MSQ[1]], [VAR[1]])
            op("vector", lambda e: e.tensor_scalar(out=VAR[0][:], in0=VAR[0][:], scalar1=GN_EPS, scalar2=None, op0=ALU.add),
               reads=[VAR[1]], writes=[VAR[1]])
            ACT(VAR[0][:], VAR[0][:], AF.Sqrt, [VAR[1]], [VAR[1]])
            op("vector", lambda e: e.reciprocal(out=VAR[0][:], in_=VAR[0][:]), reads=[VAR[1]], writes=[VAR[1]])
            TT("gpsimd", YS[0][:], YS[0][:], MEAN[0][:], ALU.subtract, [YS[1], MEAN[1]], [YS[1]])
            TT("vector", YS[0][:], YS[0][:], VAR[0][:], ALU.mult, [YS[1], VAR[1]], [YS[1]])
            TT("gpsimd", YS[0][:], YS[0][:], bc_t(pv_t[:, 7, :]), ALU.mult, [YS[1], pv_tb], [YS[1]])
            TT("gpsimd", YS[0][:], YS[0][:], bc_t(pv_t[:, 8, :]), ALU.add, [YS[1], pv_tb], [YS[1]])
            TT("vector", RK[0][:], pbv(3), xv, ALU.mult, [PB[3][1], XS[1]], [RK[1]])
            TT("gpsimd", YS[0][:], YS[0][:], RK[0][:], ALU.add, [YS[1], RK[1]], [YS[1]])
            yo_t, yo_tb = YO[c % 2]
            TT("vector", yo_t[:], YS[0][:], GS[0][:], ALU.mult, [YS[1], GS[1]], [yo_tb])
            t0 = c * CH
            op("sync", lambda e, yo_t=yo_t, t0=t0: e.dma_start(out=yo_v[:, :, t0:t0 + CH], in_=yo_t[:]),
               reads=[yo_tb], dma="yo%d" % (c % 2))

        load(0)
        if nchunks > 1:
            load(1)
        prep(0)
        for c in range(nchunks):
            if c + 2 < nchunks:
                load(c + 2)
            thunks = []
            if c + 1 < nchunks:
                S.defer = thunks
                prep(c + 1)
                S.defer = None
            body(c, thunks)


def b_consts():
    s = np.arange(CH)[:, None]
    t = np.arange(CH)[None, :]
    cst = np.zeros((64, 6, 64), np.float32)
    cst[:, 0] = (s < t)
    cst[:, 1] = (s <= t)
    cst[:, 2] = (s > t)
    cst[:, 3] = np.eye(64)
    cst[:, 4] = 1.0
    return cst


def b_inputs(zr_b, p, l, hh, nchunks):
    TP1 = nchunks * CH + 1
    Tb = zr_b.shape[1]
    hs = slice(hh * 8, hh * 8 + 8)
    rkv = np.zeros((3, 8, 64, TP1), np.float32)
    for w in range(3):
        rkv[w, :, :, 1:1 + Tb] = zr_b[w * D:(w + 1) * D].reshape(16, 64, Tb)[hs]
    zw = np.zeros((64, TP1), np.float32); zw[:, 1:1 + Tb] = zr_b[3 * D:3 * D + 64]
    za = np.zeros((64, TP1), np.float32); za[:, 1:1 + Tb] = zr_b[3 * D + 64:3 * D + 128]
    zg = np.zeros((128, TP1), np.float32); zg[:, 1:1 + Tb] = zr_b[3 * D + 128:3 * D + 256]
    tm = p["time_mix"][l]
    hv = lambda v: np.ascontiguousarray(v.reshape(16, 64)[hs].T)
    pv = np.zeros((64, 11, 8), np.float32)
    pv[:, 0] = hv(tm[0:D]); pv[:, 1] = hv(tm[D:2 * D]); pv[:, 2] = hv(tm[2 * D:3 * D])
    pv[:, 3] = hv(p["k_k"][l]); pv[:, 4] = hv(p["k_a"][l])
    pv[:, 6] = hv(p["r_k"][l].reshape(-1)); pv[:, 7] = hv(p["lnx_g"][l]); pv[:, 8] = hv(p["lnx_b"][l])
    psm = np.zeros((128, 4), np.float32)
    psm[0:64, 0] = tm[3 * D:3 * D + 64]; psm[0:64, 1] = tm[3 * D + 64:3 * D + 128]; psm[:, 2] = tm[3 * D + 128:3 * D + 256]
    wup = np.zeros((65, 8, 64), np.float32)
    wup[0:64] = p["w_up"][l].reshape(64, 16, 64)[:, hs]; wup[64] = p["w0"][l].reshape(16, 64)[hs]
    aup = np.zeros((65, 8, 64), np.float32)
    aup[0:64] = p["a_up"][l].reshape(64, 16, 64)[:, hs]; aup[64] = p["a0"][l].reshape(16, 64)[hs]
    gup = np.ascontiguousarray(p["g_up"][l].reshape(128, 16, 64)[:, hs])
    return {"rkv": rkv, "zw": zw, "za": za, "zg": zg, "pv": pv, "psm": psm, "wup": wup, "aup": aup, "gup": gup,
            "cst": b_consts()}


TOK = T // 2
HALO = 56
WIN = TOK + HALO
NBK = 416
NBLK = WIN // NBK


class TP:
    def __init__(self, cx, banks, n, slabmode=False):
        self.slabmode = slabmode
        self.q_w = "sync" if slabmode else "gpsimd"
        self.q_a = "gpsimd" if slabmode else "sync"
        self.cx = cx
        self.S = self.cx.S
        self.op = self.S.op
        self.n = n
        self.banks = banks
        self.bi = 0
        self.slabs = [cx.sb([128, 22, 128], BF16, name="slab%d" % i) for i in range(8)]
        self.si = 0
        self.stg = [cx.sb([128, n], F32, name="stg%d" % i) for i in range(4)]
        self.gi = 0
        self.onesD = cx.sb([128, 128], F32, name="onesD", const=True)
        self.op("vector", lambda e: e.memset(self.onesD[0][:], 1.0 / D), writes=[self.onesD[1]])
        self.mean = cx.sb([128, n], F32, name="ln_mean")
        self.msq = cx.sb([128, n], F32, name="ln_msq")
        self.rstd = cx.sb([128, n], F32, name="ln_rstd")
        self.tmp = [cx.sb([128, n], F32, name="tmp%d" % i) for i in range(2)]
        self.ti = 0

    def bank(self):
        b = self.banks[self.bi % 8]
        self.bi += 1
        return b

    def next_tmp(self):
        t = self.tmp[self.ti % 2]
        self.ti += 1
        return t

    def MM(self, out, lhsT, rhs, reads, writes, start, stop):
        self.op("tensor", lambda e: e.matmul(out, lhsT, rhs, start=start, stop=stop), reads=reads, writes=writes)

    def linear(self, X, KC, groups, evac):
        n = self.n
        for gi, grp in enumerate(groups):
            bks = []
            for (wv, col0) in grp:
                k = self.si % 8
                self.si += 1
                slab, slab_b = self.slabs[k]
                if self.slabmode:
                    self.op(self.q_w, lambda e, slab=slab, wv=wv, col0=col0: e.dma_start(
                        out=slab[:, 0:KC, :], in_=wv[col0 // 128]), writes=[slab_b], dma="slab%d" % k)
                else:
                    self.op(self.q_w, lambda e, slab=slab, wv=wv, col0=col0: e.dma_start(
                        out=slab[:, 0:KC, :], in_=wv[:, :, col0:col0 + 128]), writes=[slab_b], dma="slab%d" % k)
                bk = self.bank()
                for kc in range(KC):
                    self.MM(bk[0][:, 0:n], slab[:, kc, :], X[0][:, kc, :], [slab_b, X[1]], [bk[1]], kc == 0, kc == KC - 1)
                bks.append(bk)
            evac(gi, bks)

    def store(self, dram_ap, src_tile_fn):
        k = self.gi % 4
        self.gi += 1
        st, st_b = self.stg[k]
        src_tile_fn(st, st_b)
        self.op(self.q_a, lambda e: e.dma_start(out=dram_ap, in_=st[:]), reads=[st_b], dma="stg%d" % k)

    def layernorm(self, X, g_ap, b_ap, SQ, out_bf, silu=False):
        op = self.op
        n = self.n
        Xt, Xb = X
        SQt, SQb = SQ
        op("gpsimd", lambda e: e.tensor_tensor(out=SQt[:], in0=Xt[:], in1=Xt[:], op=ALU.mult), reads=[Xb], writes=[SQb])
        bm = self.bank()
        bq = self.bank()
        for dc in range(8):
            self.MM(bm[0][:, 0:n], self.onesD[0][:], Xt[:, dc, :], [self.onesD[1], Xb], [bm[1]], dc == 0, dc == 7)
        for dc in range(8):
            self.MM(bq[0][:, 0:n], self.onesD[0][:], SQt[:, dc, :], [self.onesD[1], SQb], [bq[1]], dc == 0, dc == 7)
        mean, msq, rstd = self.mean, self.msq, self.rstd
        op("scalar", lambda e: e.activation(out=mean[0][:], in_=bm[0][:, 0:n], func=AF.Copy), reads=[bm[1]], writes=[mean[1]])
        op("gpsimd", lambda e: e.tensor_tensor(out=msq[0][:], in0=mean[0][:], in1=mean[0][:], op=ALU.mult),
           reads=[mean[1]], writes=[msq[1]])
        op("vector", lambda e: e.scalar_tensor_tensor(out=rstd[0][:], in0=bq[0][:, 0:n], scalar=LN_EPS, in1=msq[0][:],
                                                      op0=ALU.add, op1=ALU.subtract), reads=[bq[1], msq[1]], writes=[rstd[1]])
        op("scalar", lambda e: e.activation(out=rstd[0][:], in_=rstd[0][:], func=AF.Sqrt), reads=[rstd[1]], writes=[rstd[1]])
        op("vector", lambda e: e.reciprocal(out=rstd[0][:], in_=rstd[0][:]), reads=[rstd[1]], writes=[rstd[1]])
        bc = lambda t: t[0][:].unsqueeze(1).to_broadcast([128, 8, n])
        pb = lambda a: a.unsqueeze(2).to_broadcast([128, 8, n])
        op("gpsimd", lambda e: e.tensor_tensor(out=Xt[:], in0=Xt[:], in1=bc(mean), op=ALU.subtract), reads=[Xb, mean[1]], writes=[Xb])
        op("vector", lambda e: e.tensor_tensor(out=Xt[:], in0=Xt[:], in1=bc(rstd), op=ALU.mult), reads=[Xb, rstd[1]], writes=[Xb])
        op("gpsimd", lambda e: e.tensor_tensor(out=Xt[:], in0=Xt[:], in1=pb(g_ap), op=ALU.mult), reads=[Xb, self.lnp[1]], writes=[Xb])
        op("vector", lambda e: e.tensor_tensor(out=Xt[:], in0=Xt[:], in1=pb(b_ap), op=ALU.add), reads=[Xb, self.lnp[1]], writes=[Xb])
        if out_bf is not None:
            f = AF.Silu if silu else AF.Copy
            op("scalar", lambda e: e.activation(out=out_bf[0][:], in_=Xt[:], func=f), reads=[Xb], writes=[out_bf[1]])

    def ffn(self, Xb16, ACTt, wg_v, wu_v, wd_v, Hres, Xout):
        op = self.op
        n = self.n

        def ev1(fc, bks):
            tmp = self.next_tmp()
            op("scalar", lambda e: e.activation(out=tmp[0][:], in_=bks[0][0][:, 0:n], func=AF.Silu),
               reads=[bks[0][1]], writes=[tmp[1]])
            op("vector", lambda e: e.tensor_tensor(out=ACTt[0][:, fc, :], in0=tmp[0][:], in1=bks[1][0][:, 0:n], op=ALU.mult),
               reads=[tmp[1], bks[1][1]], writes=[ACTt[1]])

        self.linear(Xb16, 8, [[(wg_v, fc * 128), (wu_v, fc * 128)] for fc in range(DFF // 128)], ev1)

        def ev2(dc, bks):
            tmp = self.next_tmp()
            op("scalar", lambda e: e.activation(out=tmp[0][:], in_=bks[0][0][:, 0:n], func=AF.Copy, scale=0.5),
               reads=[bks[0][1]], writes=[tmp[1]])
            op("vector", lambda e: e.scalar_tensor_tensor(out=Xout[0][:, dc, :], in0=Hres[0][:, dc, :], scalar=float(ALPHA),
                                                          in1=tmp[0][:], op0=ALU.mult, op1=ALU.add),
               reads=[Hres[1], tmp[1]], writes=[Xout[1]])

        self.linear(ACTt, DFF // 128, [[(wd_v, dc * 128)] for dc in range(8)], ev2)

    def finish(self):
        S = self.S
        for k in S.dsem:
            S._wait("sync", (S.dsem[k], S.dcnt[k]))
        S.emit()


def wview(ap):
    return ap.rearrange("(kc p) o -> p kc o", p=128)


def aview(ap):
    return ap.rearrange("(c p) t -> p c t", p=128)


def build_A(nblk=NBLK, n=NBK):
    W = nblk * n
    nc = bass.Bass("TRN2", target_bir_lowering=False)
    stack = ExitStack()
    with stack:
        cx = Ctx(nc, stack)
        banks = [cx.ps([128, 512], name="bk%d" % i) for i in range(8)]
        tp = TP(cx, banks, n)
        dr = lambda name, shape, kind="ExternalInput": nc.dram_tensor(name, list(shape), F32, kind=kind).ap()
        d = {"hT": dr("hT", [D, W]), "wg": dr("wg", [D, DFF]), "wu": dr("wu", [D, DFF]), "wd": dr("wd", [DFF, D]),
             "w_in": dr("w_in", [D, NIN]), "cwo": dr("cwo", [D, D]), "convw": dr("convw", [D, CW]),
             "lnp": dr("lnp", [128, 5, 8]), "flag": dr("flag", [128, 1]),
             "h1": dr("h1", [D, W], "ExternalOutput"), "zr": dr("zr", [RWKV_COLS, W], "ExternalOutput"),
             "gr": dr("gr", [D, W], "ExternalOutput"), "cp": dr("cp", [D, W], "ExternalOutput")}
        emit_A(tp, d, nblk, n)
        tp.finish()
    return nc


def emit_A(tp, d, nblk, n, use_flag=True):
    if True:
        cx, op = tp.cx, tp.op
        hT_in = aview(d["hT"])
        wv_ = (lambda a: a) if tp.slabmode else wview
        wg_v = wv_(d["wg"]); wu_v = wv_(d["wu"]); wd_v = wv_(d["wd"])
        win_v = wv_(d["w_in"]); cwo_v = wv_(d["cwo"])
        convw = d["convw"]; lnp_d = d["lnp"]
        h1_out = aview(d["h1"])
        zr_out = d["zr"]; gr_out = d["gr"]; cp_out = d["cp"]

        lnp = cx.sb([128, 5, 8], name="lnp", const=True); tp.lnp = lnp
        cw = cx.sb([128, 8, CW], name="cw", const=True)
        op("sync", lambda e: e.dma_start(out=lnp[0][:], in_=lnp_d), writes=[lnp[1]], dma="c0")
        op("sync", lambda e: e.dma_start(out=cw[0][:], in_=convw.rearrange("(c p) j -> p c j", p=128)), writes=[cw[1]], dma="c1")
        if use_flag:
            flag = cx.sb([128, 1], name="flag", const=True)
            op("sync", lambda e: e.dma_start(out=flag[0][:], in_=d["flag"]), writes=[flag[1]], dma="c2")

        R1 = cx.sb([128, 8, n], name="R1"); R2 = cx.sb([128, 8, n], name="R2")
        hTb = cx.sb([128, 8, n], BF16, name="hTb"); h1b = cx.sb([128, 8, n], BF16, name="h1b")
        ACTt = cx.sb([128, DFF // 128, n], BF16, name="ACTt")
        U = cx.sb([128, 8, CW - 1 + n], name="U")
        GC = cx.sb([128, 8, n], name="GC")
        UC = cx.sb([128, 8, n], BF16, name="UC")
        op("vector", lambda e: e.memset(U[0][:], 0.0), writes=[U[1]])

        for blk in range(nblk):
            t0 = blk * n
            op(tp.q_a, lambda e, t0=t0: e.dma_start(out=R1[0][:], in_=hT_in[:, :, t0:t0 + n]), writes=[R1[1]], dma="ldh")
            op("scalar", lambda e: e.activation(out=hTb[0][:], in_=R1[0][:], func=AF.Copy), reads=[R1[1]], writes=[hTb[1]])
            tp.ffn(hTb, ACTt, wg_v, wu_v, wd_v, R1, R2)
            tp.layernorm(R2, lnp[0][:, 0, :], lnp[0][:, 1, :], R1, h1b)
            op(tp.q_a, lambda e, t0=t0: e.dma_start(out=h1_out[:, :, t0:t0 + n], in_=R2[0][:]), reads=[R2[1]], dma="sth1")

            def ev_glu(c, bks):
                tmp = tp.next_tmp()
                op("scalar", lambda e: e.activation(out=tmp[0][:], in_=bks[1][0][:, 0:n], func=AF.Sigmoid),
                   reads=[bks[1][1]], writes=[tmp[1]])
                op("vector", lambda e: e.tensor_tensor(out=U[0][:, c, CW - 1:CW - 1 + n], in0=tmp[0][:], in1=bks[0][0][:, 0:n],
                                                       op=ALU.mult), reads=[tmp[1], bks[0][1]], writes=[U[1]])
            tp.linear(h1b, 8, [[(win_v, c * 128), (win_v, D + c * 128)] for c in range(8)], ev_glu)

            def ev_zr(j, bks, t0=t0):
                tp.store(zr_out[j * 128:(j + 1) * 128, t0:t0 + n],
                         lambda st, st_b: op("scalar", lambda e: e.activation(out=st[:], in_=bks[0][0][:, 0:n], func=AF.Copy),
                                             reads=[bks[0][1]], writes=[st_b]))
            tp.linear(h1b, 8, [[(win_v, 2 * D + j * 128)] for j in range(RWKV_COLS // 128)], ev_zr)

            def ev_gc(c, bks):
                op("scalar", lambda e: e.activation(out=GC[0][:, c, :], in_=bks[0][0][:, 0:n], func=AF.Sigmoid),
                   reads=[bks[0][1]], writes=[GC[1]])
            tp.linear(h1b, 8, [[(win_v, 2 * D + RWKV_COLS + c * 128)] for c in range(8)], ev_gc)

            def ev_gr(c, bks, t0=t0):
                tp.store(gr_out[c * 128:(c + 1) * 128, t0:t0 + n],
                         lambda st, st_b: op("scalar", lambda e: e.activation(out=st[:], in_=bks[0][0][:, 0:n], func=AF.Sigmoid),
                                             reads=[bks[0][1]], writes=[st_b]))
            tp.linear(h1b, 8, [[(win_v, 3 * D + RWKV_COLS + c * 128)] for c in range(8)], ev_gr)

            if blk == 0 and use_flag:
                op("vector", lambda e: e.tensor_scalar(out=U[0][:, :, CW - 1:CW - 1 + HALO], in0=U[0][:, :, CW - 1:CW - 1 + HALO],
                                                       scalar1=flag[0][:, 0:1], scalar2=None, op0=ALU.mult),
                   reads=[U[1], flag[1]], writes=[U[1]])
            for c in range(8):
                op("vector", lambda e, c=c: e.tensor_scalar(out=R1[0][:, c, :], in0=U[0][:, c, 0:n], scalar1=cw[0][:, c, 0:1],
                                                            scalar2=lnp[0][:, 2, c:c + 1], op0=ALU.mult, op1=ALU.add),
                   reads=[U[1], cw[1], lnp[1]], writes=[R1[1]])
                for j in range(1, CW):
                    op("vector", lambda e, c=c, j=j: e.scalar_tensor_tensor(
                        out=R1[0][:, c, :], in0=U[0][:, c, j:j + n], scalar=cw[0][:, c, j:j + 1], in1=R1[0][:, c, :],
                        op0=ALU.mult, op1=ALU.add), reads=[U[1], cw[1], R1[1]], writes=[R1[1]])
            op("scalar", lambda e: e.activation(out=U[0][:, :, 0:CW - 1], in_=U[0][:, :, n:n + CW - 1], func=AF.Copy),
               reads=[U[1]], writes=[U[1]])
            tp.layernorm(R1, lnp[0][:, 3, :], lnp[0][:, 4, :], R2, UC, silu=True)

            def ev_co(dc, bks, t0=t0):
                tp.store(cp_out[dc * 128:(dc + 1) * 128, t0:t0 + n],
                         lambda st, st_b: op("vector", lambda e: e.tensor_tensor(out=st[:], in0=GC[0][:, dc, :], in1=bks[0][0][:, 0:n],
                                                                                 op=ALU.mult),
                                             reads=[GC[1], bks[0][1]], writes=[st_b]))
            tp.linear(UC, 8, [[(cwo_v, dc * 128)] for dc in range(8)], ev_co)


def build_C(nblk=NBLK, n=NBK):
    W = nblk * n
    nc = bass.Bass("TRN2", target_bir_lowering=False)
    stack = ExitStack()
    with stack:
        cx = Ctx(nc, stack)
        banks = [cx.ps([128, 512], name="bk%d" % i) for i in range(8)]
        tp = TP(cx, banks, n)
        dr = lambda name, shape, kind="ExternalInput": nc.dram_tensor(name, list(shape), F32, kind=kind).ap()
        d = {"yB": dr("yB", [D, W]), "h1": dr("h1", [D, W]), "gr": dr("gr", [D, W]), "cp": dr("cp", [D, W]),
             "rwo": dr("rwo", [D, D]), "wout": dr("wout", [D, D]), "wg": dr("wg", [D, DFF]), "wu": dr("wu", [D, DFF]),
             "wd": dr("wd", [DFF, D]), "lnp": dr("lnp", [128, 4, 8]), "hout": dr("hout", [D, W], "ExternalOutput")}
        emit_C(tp, d, nblk, n)
        tp.finish()
    return nc


def emit_C(tp, d, nblk, n):
    if True:
        cx, op = tp.cx, tp.op
        yB_in = aview(d["yB"]); h1_in = aview(d["h1"]); gr_in = aview(d["gr"]); cp_in = aview(d["cp"])
        wv_ = (lambda a: a) if tp.slabmode else wview
        rwo_v = wv_(d["rwo"]); wout_v = wv_(d["wout"])
        wg_v = wv_(d["wg"]); wu_v = wv_(d["wu"]); wd_v = wv_(d["wd"])
        lnp_d = d["lnp"]
        h_out = aview(d["hout"])
        lnp = cx.sb([128, 4, 8], name="lnp", const=True); tp.lnp = lnp
        op("sync", lambda e: e.dma_start(out=lnp[0][:], in_=lnp_d), writes=[lnp[1]], dma="c0")
        R1 = cx.sb([128, 8, n], name="R1"); R2 = cx.sb([128, 8, n], name="R2")
        R3 = cx.sb([128, 8, n], name="R3"); R4 = cx.sb([128, 8, n], name="R4")
        YB = cx.sb([128, 8, n], BF16, name="YB"); HM = cx.sb([128, 8, n], BF16, name="HM")
        h2b = cx.sb([128, 8, n], BF16, name="h2b")
        ACTt = cx.sb([128, DFF // 128, n], BF16, name="ACTt")
        for blk in range(nblk):
            t0 = blk * n
            sl = slice(t0, t0 + n)
            op("gpsimd", lambda e, sl=sl: e.dma_start(out=YB[0][:], in_=yB_in[:, :, sl]), writes=[YB[1]], dma="ldy")
            op(tp.q_a, lambda e, sl=sl: e.dma_start(out=R1[0][:], in_=h1_in[:, :, sl]), writes=[R1[1]], dma="ld1")
            op(tp.q_a, lambda e, sl=sl: e.dma_start(out=R3[0][:], in_=gr_in[:, :, sl]), writes=[R3[1]], dma="ld3")
            op(tp.q_a, lambda e, sl=sl: e.dma_start(out=R4[0][:], in_=cp_in[:, :, sl]), writes=[R4[1]], dma="ld4")

            def ev_r(dc, bks):
                tmp = tp.next_tmp()
                op("vector", lambda e: e.tensor_tensor(out=tmp[0][:], in0=R3[0][:, dc, :], in1=bks[0][0][:, 0:n], op=ALU.mult),
                   reads=[R3[1], bks[0][1]], writes=[tmp[1]])
                op(HM_ENG, lambda e: e.tensor_tensor(out=HM[0][:, dc, :], in0=tmp[0][:], in1=R4[0][:, dc, :], op=ALU.add),
                   reads=[tmp[1], R4[1]], writes=[HM[1]])
            tp.linear(YB, 8, [[(rwo_v, dc * 128)] for dc in range(8)], ev_r)

            def ev_m(dc, bks):
                op("vector", lambda e: e.scalar_tensor_tensor(out=R2[0][:, dc, :], in0=R1[0][:, dc, :], scalar=float(ALPHA),
                                                              in1=bks[0][0][:, 0:n], op0=ALU.mult, op1=ALU.add),
                   reads=[R1[1], bks[0][1]], writes=[R2[1]])
            if C_STAGE >= 2:
                tp.linear(HM, 8, [[(wout_v, dc * 128)] for dc in range(8)], ev_m)
                tp.layernorm(R2, lnp[0][:, 0, :], lnp[0][:, 1, :], R3, h2b)
            if C_STAGE >= 3:
                tp.ffn(h2b, ACTt, wg_v, wu_v, wd_v, R2, R4)
                tp.layernorm(R4, lnp[0][:, 2, :], lnp[0][:, 3, :], R3, None)
            op(tp.q_a, lambda e, sl=sl: e.dma_start(out=h_out[:, :, sl], in_=R4[0][:]), reads=[R4[1]], dma="sth")


def chunkvec(v):
    return np.ascontiguousarray(v.reshape(8, 128).T)


def a_inputs(hT_win, p, l, flag):
    lnp = np.stack([chunkvec(p["ln1_g"][l]), chunkvec(p["ln1_b"][l]), chunkvec(p["conv_b"][l]),
                    chunkvec(p["conv_ln_g"][l]), chunkvec(p["conv_ln_b"][l])], axis=1).astype(np.float32)
    return {"hT": hT_win, "wg": p["ffn1_wg"][l], "wu": p["ffn1_wu"][l], "wd": p["ffn1_wd"][l], "w_in": p["w_in"][l],
            "cwo": p["conv_wo"][l], "convw": np.ascontiguousarray(p["conv_w"][l].T), "lnp": np.ascontiguousarray(lnp),
            "flag": np.full((128, 1), flag, np.float32)}


def c_inputs(yB_win, h1_win, gr_win, cp_win, p, l):
    lnp = np.stack([chunkvec(p["ln2_g"][l]), chunkvec(p["ln2_b"][l]), chunkvec(p["ln3_g"][l]), chunkvec(p["ln3_b"][l])],
                   axis=1).astype(np.float32)
    return {"yB": yB_win, "h1": h1_win, "gr": gr_win, "cp": cp_win, "rwo": p["rwkv_wo"][l], "wout": p["w_out"][l],
            "wg": p["ffn2_wg"][l], "wu": p["ffn2_wu"][l], "wd": p["ffn2_wd"][l], "lnp": np.ascontiguousarray(lnp)}


_PROGS = {}


def _prog(name):
    if name not in _PROGS:
        _PROGS[name] = {"A": build_A, "C": build_C, "B": lambda: build_B(TPB // CH)}[name]()
    return _PROGS[name]


def _windows(full):
    outs = []
    for b in range(NB):
        for j in range(2):
            lo = j * TOK - HALO
            w = np.zeros((full.shape[1], WIN), np.float32)
            s = max(lo, 0)
            w[:, s - lo:] = full[b][:, s:(j + 1) * TOK]
            outs.append(w)
    return outs


def _unwindow(res, key, C):
    full = np.empty((NB, C, T), np.float32)
    for b in range(NB):
        for j in range(2):
            full[b][:, j * TOK:(j + 1) * TOK] = res[2 * b + j][key][:, HALO:]
    return full


def kernel_unfused(**inp):
    p = {k: np.ascontiguousarray(np.asarray(v, dtype=np.float32)) for k, v in inp.items()}
    x = p["x"]
    cores = list(range(8))
    h = np.empty((NB, D, T), np.float32)
    for b in range(NB):
        h[b][:, :NMETA] = p["meta_tokens"].T
        h[b][:, NMETA:] = x[b].T
    for l in range(DEPTH):
        hw = _windows(h)
        in_maps = [a_inputs(hw[c], p, l, float(c % 2)) for c in cores]
        rA = run_bass_kernel_spmd(_prog("A"), in_maps, core_ids=cores).results
        h1 = _unwindow(rA, "h1", D)
        zr = _unwindow(rA, "zr", RWKV_COLS)
        gr = _unwindow(rA, "gr", D)
        cp = _unwindow(rA, "cp", D)
        del rA, in_maps, hw
        in_maps = [b_inputs(zr[c // 2], p, l, c % 2, TPB // CH) for c in cores]
        del zr
        rB = run_bass_kernel_spmd(_prog("B"), in_maps, core_ids=cores).results
        yB = np.empty((NB, D, T), np.float32)
        for c in cores:
            yB[c // 2][(c % 2) * 512:(c % 2 + 1) * 512] = rB[c]["yo"].reshape(512, TPB)[:, :T]
        del rB, in_maps
        yw, h1w, grw, cpw = _windows(yB), _windows(h1), _windows(gr), _windows(cp)
        in_maps = [c_inputs(yw[c], h1w[c], grw[c], cpw[c], p, l) for c in cores]
        rC = run_bass_kernel_spmd(_prog("C"), in_maps, core_ids=cores).results
        h = _unwindow(rC, "hout", D)
        del rC, in_maps
    out = np.empty((NB, SEQ, D), np.float32)
    for b in range(NB):
        out[b] = h[b][:, NMETA:].T
    return out


FBLK = 20
WF = FBLK * NBK
NCHF = TPB // CH

W_SHAPES = [("ffn1_wg", [DEPTH, D, DFF]), ("ffn1_wu", [DEPTH, D, DFF]), ("ffn1_wd", [DEPTH, DFF, D]),
            ("w_in", [DEPTH, D, NIN]), ("conv_wo", [DEPTH, D, D]), ("rwkv_wo", [DEPTH, D, D]), ("w_out", [DEPTH, D, D]),
            ("ffn2_wg", [DEPTH, D, DFF]), ("ffn2_wu", [DEPTH, D, DFF]), ("ffn2_wd", [DEPTH, DFF, D])]


def build_F(depth=DEPTH, fblk=FBLK, nchf=NCHF):
    wf = fblk * NBK
    nc = bass.Bass("TRN2", target_bir_lowering=False)
    g = ExitStack()
    with g:
        S = Sched(nc, g)
        gcx = Ctx(nc, g, S)
        banks = [gcx.ps([128, 512], name="bk%d" % i) for i in range(8)]
        dr = lambda name, shape, kind="ExternalInput": nc.dram_tensor(name, list(shape), F32, kind=kind).ap()
        xT = dr("xT", [D, wf])
        Wd = {k: dr(k, sh) for k, sh in W_SHAPES}
        lnpA = dr("lnpA", [DEPTH, 128, 5, 8]); lnpC = dr("lnpC", [DEPTH, 128, 4, 8]); convw = dr("convw", [DEPTH, D, CW])
        pvB = dr("pvB", [DEPTH, 2, 64, 11, 8]); psmB = dr("psmB", [DEPTH, 128, 4])
        wupB = dr("wupB", [DEPTH, 2, 65, 8, 64]); aupB = dr("aupB", [DEPTH, 2, 65, 8, 64]); gupB = dr("gupB", [DEPTH, 2, 128, 8, 64])
        cst = dr("cst", [64, 6, 64])
        out = dr("out", [D, wf], "ExternalOutput")
        hbuf = dr("hbuf", [D, wf], "Internal"); h1buf = dr("h1buf", [D, wf], "Internal")
        zrbuf = dr("zrbuf", [RWKV_COLS, wf + 1], "Internal")
        grbuf = dr("grbuf", [D, wf], "Internal"); cpbuf = dr("cpbuf", [D, wf], "Internal"); ybuf = dr("ybuf", [D, wf], "Internal")
        with ExitStack() as st:
            cx = Ctx(nc, st, S)
            zt = cx.sb([128, wf - nchf * CH if wf > nchf * CH else 64], name="zt")
            S.op("vector", lambda e: e.memset(zt[0][:], 0.0), writes=[zt[1]])
            for j in range(RWKV_COLS // 128):
                S.op("sync", lambda e, j=j: e.dma_start(out=zrbuf[j * 128:(j + 1) * 128, 0:1], in_=zt[0][:, 0:1], allow_slow_non_contiguous=True),
                     reads=[zt[1]], dma="zi")
            if wf > nchf * CH:
                for c in range(8):
                    S.op("sync", lambda e, c=c: e.dma_start(out=ybuf[c * 128:(c + 1) * 128, nchf * CH:wf], in_=zt[0][:]),
                         reads=[zt[1]], dma="zi")
            S.barrier()
        Wb = {}
        for k, sh in W_SHAPES:
            kc, oc = sh[1] // 128, sh[2] // 128
            Wb[k] = nc.dram_tensor("wb_" + k, [oc, 128, kc, 128], BF16, kind="Internal").ap()
        for l in range(depth):
            for k, sh in W_SHAPES:
                wv = wview(Wd[k][l])
                for oc in range(sh[2] // 128):
                    S.op("gpsimd", lambda e, wb=Wb[k], wv=wv, oc=oc: e.dma_start(out=wb[oc], in_=wv[:, :, oc * 128:(oc + 1) * 128]),
                         dma="cv")
            S.barrier()
            with ExitStack() as st:
                cx = Ctx(nc, st, S)
                tp = TP(cx, banks, NBK, slabmode=True)
                dA = {"hT": xT if l == 0 else hbuf, "wg": Wb["ffn1_wg"], "wu": Wb["ffn1_wu"], "wd": Wb["ffn1_wd"],
                      "w_in": Wb["w_in"], "cwo": Wb["conv_wo"], "convw": convw[l], "lnp": lnpA[l],
                      "h1": h1buf, "zr": zrbuf[:, 1:1 + wf], "gr": grbuf, "cp": cpbuf}
                emit_A(tp, dA, fblk, NBK, use_flag=False)
                S.barrier()
            for hh in range(2):
                with ExitStack() as st:
                    cx = Ctx(nc, st, S)
                    dB = {"rkv_v": zrbuf[0:3 * D, :].rearrange("(w hq c) t -> c w hq t", w=3, hq=NH, c=HS)[:, :, hh * 8:(hh + 1) * 8, :],
                          "zw": zrbuf[3 * D:3 * D + 64, :], "za": zrbuf[3 * D + 64:3 * D + 128, :], "zg": zrbuf[3 * D + 128:3 * D + 256, :],
                          "pv": pvB[l, hh], "psm": psmB[l], "wup": wupB[l, hh], "aup": aupB[l, hh], "gup": gupB[l, hh], "cst": cst,
                          "yo_v": ybuf.rearrange("(hq c) t -> c hq t", c=HS)[:, hh * 8:(hh + 1) * 8, :]}
                    emit_B(cx, banks, dB, nchf)
                    S.barrier()
            with ExitStack() as st:
                cx = Ctx(nc, st, S)
                tp = TP(cx, banks, NBK, slabmode=True)
                dC = {"yB": ybuf, "h1": h1buf, "gr": grbuf, "cp": cpbuf, "rwo": Wb["rwkv_wo"], "wout": Wb["w_out"],
                      "wg": Wb["ffn2_wg"], "wu": Wb["ffn2_wu"], "wd": Wb["ffn2_wd"], "lnp": lnpC[l],
                      "hout": out if l == depth - 1 else hbuf}
                emit_C(tp, dC, fblk, NBK)
                S.barrier()
        S.emit()
    return nc


def f_shared_inputs(p):
    sh = {k: p[k] for k, _ in W_SHAPES}
    sh["lnpA"] = np.ascontiguousarray(np.stack([a_inputs(None, p, l, 0.0)["lnp"] for l in range(DEPTH)]))
    sh["lnpC"] = np.ascontiguousarray(np.stack([np.stack(
        [chunkvec(p["ln2_g"][l]), chunkvec(p["ln2_b"][l]), chunkvec(p["ln3_g"][l]), chunkvec(p["ln3_b"][l])], axis=1)
        for l in range(DEPTH)]).astype(np.float32))
    sh["convw"] = np.ascontiguousarray(np.transpose(p["conv_w"], (0, 2, 1)))
    dummy = np.zeros((RWKV_COLS, 1), np.float32)
    bi = [[b_inputs(dummy, p, l, hh, 1) for hh in range(2)] for l in range(DEPTH)]
    sh["pvB"] = np.ascontiguousarray(np.stack([np.stack([bi[l][hh]["pv"] for hh in range(2)]) for l in range(DEPTH)]))
    sh["psmB"] = np.ascontiguousarray(np.stack([bi[l][0]["psm"] for l in range(DEPTH)]))
    for k, kk in (("wupB", "wup"), ("aupB", "aup"), ("gupB", "gup")):
        sh[k] = np.ascontiguousarray(np.stack([np.stack([bi[l][hh][kk] for hh in range(2)]) for l in range(DEPTH)]))
    sh["cst"] = b_consts()
    return sh


def kernel_fused(**inp):
    p = {k: np.ascontiguousarray(np.asarray(v, dtype=np.float32)) for k, v in inp.items()}
    sh = f_shared_inputs(p)
    in_maps = []
    for b in range(NB):
        xT = np.zeros((D, WF), np.float32)
        xT[:, :NMETA] = p["meta_tokens"].T
        xT[:, NMETA:T] = p["x"][b].T
        m = dict(sh)
        m["xT"] = xT
        in_maps.append(m)
    if "F" not in _PROGS:
        _PROGS["F"] = build_F()
    res = run_bass_kernel_spmd(_PROGS["F"], in_maps, core_ids=list(range(NB))).results
    out = np.empty((NB, SEQ, D), np.float32)
    for b in range(NB):
        out[b] = res[b]["out"][:, NMETA:T].T
    return out


def kernel(**inp):
    return kernel_fused(**inp)
```

```python
import numpy as np
from contextlib import ExitStack
import concourse.bass as bass
import concourse.mybir as mybir
from concourse.bass_utils import run_bass_kernel_spmd

F32 = mybir.dt.float32
BF16 = mybir.dt.bfloat16
AF = mybir.ActivationFunctionType
ALU = mybir.AluOpType

D = 1024
NB = 4
SEQ = 8192
NMETA = 16
T = SEQ + NMETA
DEPTH = 4
DFF = 2816
CW = 31
NH = 16
HS = 64
RW, RA, RG = 64, 64, 128
RWKV_COLS = 3 * D + RW + RA + RG
NIN = 2 * D + RWKV_COLS + 2 * D
LN_EPS = 1e-5
GN_EPS = 64e-5
ALPHA = (2 * DEPTH) ** 0.25
C0 = float(np.exp(-0.5))
CH = 64
TPB = 8256

SAME_ENGINE_SYNC = False
HM_ENG = 'gpsimd'
C_STAGE = 3


class Buf:
    __slots__ = ("name", "w", "r", "const")

    def __init__(self, name, const=False):
        self.name = name
        self.w = None
        self.r = {}
        self.const = const


class Sched:
    ENGS = ("sync", "gpsimd", "tensor", "scalar", "vector")

    def __init__(self, nc, stack):
        self.nc = nc
        self.stack = stack
        self.streams = {e: [] for e in self.ENGS}
        self.esem = {e: stack.enter_context(nc.semaphore("es_" + e)) for e in self.ENGS}
        self.cnt = {e: 0 for e in self.ENGS}
        self.seen = {e: {} for e in self.ENGS}
        self.dsem = {}
        self.dcnt = {}
        self.semobj = {}
        for e in self.ENGS:
            self.semobj[id(self.esem[e])] = self.esem[e]

    def _wait(self, eng, ev):
        if ev is None:
            return
        sem, val = ev
        own = sem is self.esem[eng]
        if own and (eng == "tensor" or eng == "sync" or not SAME_ENGINE_SYNC):
            return
        k = id(sem)
        if self.seen[eng].get(k, 0) >= val:
            return
        self.seen[eng][k] = val
        self.streams[eng].append(("w", sem, val))

    defer = None

    def flush(self, thunks):
        for t in thunks:
            self.op(*t)

    def op(self, eng, fn, reads=(), writes=(), dma=None):
        if self.defer is not None:
            self.defer.append((eng, fn, list(reads), list(writes), dma))
            return None
        deps = []
        for b in reads:
            deps.append(b.w)
        for b in writes:
            deps.append(b.w)
            deps.extend(b.r.values())
        for ev in deps:
            self._wait(eng, ev)
        if dma is not None:
            if dma not in self.dsem:
                self.dsem[dma] = self.stack.enter_context(self.nc.semaphore("ds_%s" % dma))
                self.dcnt[dma] = 0
            self.dcnt[dma] += 16
            ev = (self.dsem[dma], self.dcnt[dma])
            inc = 16
        else:
            self.cnt[eng] += 1
            ev = (self.esem[eng], self.cnt[eng])
            inc = 1
        self.streams[eng].append(("o", fn, ev[0], inc))
        for b in reads:
            if not b.const:
                k = id(ev[0])
                b.r[k] = ev
        for b in writes:
            b.w = ev
            b.r = {}
        return ev

    def barrier(self):
        for e in self.ENGS:
            for x in self.ENGS:
                if x != e and self.cnt[x] > 0:
                    self._wait(e, (self.esem[x], self.cnt[x]))
            for k in self.dsem:
                self._wait(e, (self.dsem[k], self.dcnt[k]))

    def final_wait(self, eng, bufs):
        for b in bufs:
            self._wait(eng, b.w)

    def emit(self):
        nc = self.nc
        streams = self.streams

        def replay(e, st):
            pend = None
            for it in st:
                if it[0] == "w":
                    if pend is not None:
                        e.wait_ge(pend[1], pend[2])
                    pend = it
                else:
                    ins = it[1](e)
                    if pend is not None:
                        ins = ins._wait_ge(pend[1], pend[2])
                        pend = None
                    ins.then_inc(it[2], it[3])
            if pend is not None:
                e.wait_ge(pend[1], pend[2])

        with nc.Block() as block:
            @block.sync
            def _(e):
                replay(e, streams["sync"])

            @block.gpsimd
            def _(e):
                replay(e, streams["gpsimd"])

            @block.tensor
            def _(e):
                replay(e, streams["tensor"])

            @block.scalar
            def _(e):
                replay(e, streams["scalar"])

            @block.vector
            def _(e):
                replay(e, streams["vector"])


class Ctx:
    UID = [0]

    def __init__(self, nc, stack, S=None):
        self.nc = nc
        self.stack = stack
        self.S = S if S is not None else Sched(nc, stack)
        self.n = 0

    def sb(self, shape, dt=F32, name=None, const=False):
        self.n += 1
        Ctx.UID[0] += 1
        name = "s%d_" % Ctx.UID[0] + (name or "t%d" % self.n)
        t = self.stack.enter_context(self.nc.sbuf_tensor(name, list(shape), dt))
        return t, Buf(name, const)

    def ps(self, shape, dt=F32, name=None):
        self.n += 1
        Ctx.UID[0] += 1
        name = "p%d_" % Ctx.UID[0] + (name or "p%d" % self.n)
        t = self.stack.enter_context(self.nc.psum_tensor(name, list(shape), dt))
        return t, Buf(name)

    def dram(self, name, shape, dt=F32, kind="ExternalInput"):
        return self.nc.dram_tensor(name, list(shape), dt, kind=kind).ap(), Buf(name)


def build_B(nchunks):
    TP1 = nchunks * CH + 1
    nc = bass.Bass("TRN2", target_bir_lowering=False)
    stack = ExitStack()
    with stack:
        cx = Ctx(nc, stack)
        H = 8
        dr = lambda name, shape, kind="ExternalInput": nc.dram_tensor(name, list(shape), F32, kind=kind).ap()
        d = {"rkv_v": dr("rkv", [3, H, 64, TP1]).rearrange("w h c t -> c w h t"),
             "zw": dr("zw", [64, TP1]), "za": dr("za", [64, TP1]), "zg": dr("zg", [128, TP1]),
             "pv": dr("pv", [64, 11, H]), "psm": dr("psm", [128, 4]), "wup": dr("wup", [65, H, 64]),
             "aup": dr("aup", [65, H, 64]), "gup": dr("gup", [128, H, 64]), "cst": dr("cst", [64, 6, 64]),
             "yo_v": dr("yo", [H, 64, nchunks * CH], "ExternalOutput").rearrange("h c t -> c h t")}
        PB = [cx.ps([128, 512], name="pb%d" % i) for i in range(8)]
        emit_B(cx, PB, d, nchunks)
        S = cx.S
        for k in S.dsem:
            S._wait("sync", (S.dsem[k], S.dcnt[k]))
        S.emit()
    return nc


def emit_B(cx, PB, d, nchunks):
    if True:
        S = cx.S
        op = S.op
        H = 8
        rkv_v, zw, za, zg, pv, ps_, wup, aup, gup, cst, yo_v = (d[k] for k in (
            "rkv_v", "zw", "za", "zg", "pv", "psm", "wup", "aup", "gup", "cst", "yo_v"))
        rkv_b = zw_b = za_b = zg_b = None

        pv_t, pv_tb = cx.sb([64, 11, H], name="pv_t", const=True)
        psm_t, psm_tb = cx.sb([128, 4], name="psm_t", const=True)
        wup_t, wup_tb = cx.sb([65, H, 64], name="wup_t", const=True)
        aup_t, aup_tb = cx.sb([65, H, 64], name="aup_t", const=True)
        gup_t, gup_tb = cx.sb([128, H, 64], name="gup_t", const=True)
        cst_t, cst_tb = cx.sb([64, 6, 64], name="cst_t", const=True)
        rst_t, rst_tb = cx.sb([64, H, 64], name="rst_t", const=True)
        for (dst, dstb, src, key) in ((pv_t, pv_tb, pv, "c0"), (psm_t, psm_tb, ps_, "c1"), (wup_t, wup_tb, wup, "c2"),
                                      (aup_t, aup_tb, aup, "c3"), (gup_t, gup_tb, gup, "c4"), (cst_t, cst_tb, cst, "c5")):
            op("sync", (lambda e, d=dst, s=src: e.dma_start(out=d[:], in_=s)), writes=[dstb], dma=key)
        op("vector", lambda e: e.memset(rst_t[:], 1.0), writes=[rst_tb])
        op("vector", lambda e: e.memset(rst_t[:, :, 0:1], 0.0), writes=[rst_tb])

        def bc_h(ap2d):
            return ap2d.unsqueeze(1).to_broadcast([64, H, 64])

        def bc_t(ap2d):
            return ap2d.unsqueeze(2).to_broadcast([64, H, 64])

        m_su, m_u, m_sl, ident, ones = (cst_t[:, i, :] for i in range(5))
        m_uu = cst_t[:, 0:2, :]
        msw_t, msw_tb = cx.sb([64, 2, 64], name="msw_t", const=True)
        op("vector", lambda e: e.tensor_copy(out=msw_t[:, 0, :], in_=cst_t[:, 1, :]), reads=[cst_tb], writes=[msw_tb])
        op("vector", lambda e: e.tensor_copy(out=msw_t[:, 1, :], in_=cst_t[:, 0, :]), reads=[cst_tb], writes=[msw_tb])
        onesm_t, onesm_tb = cx.sb([64, 64], name="onesm_t", const=True)
        op("vector", lambda e: e.tensor_scalar(out=onesm_t[:], in0=cst_t[:, 4, :], scalar1=1.0 / 64, scalar2=None, op0=ALU.mult),
           reads=[cst_tb], writes=[onesm_tb])

        def sbn(shape, name):
            return cx.sb(shape, name=name)

        NBUF = 2
        Zin = [sbn([64, 3, H, CH + 1], "Zin%d" % i) for i in range(NBUF)]
        ZWi = [sbn([64, CH + 1], "ZWi%d" % i) for i in range(NBUF)]
        ZAi = [sbn([64, CH + 1], "ZAi%d" % i) for i in range(NBUF)]
        ZGi = [sbn([128, CH + 1], "ZGi%d" % i) for i in range(NBUF)]
        Dt = sbn([64, 3, H, CH], "Dt")
        XSs = [sbn([64, 3, H, CH], "XS%d" % i) for i in range(2)]
        Dw = sbn([64, CH], "Dw"); Da = sbn([64, CH], "Da"); Dg = sbn([128, CH], "Dg")
        TW = sbn([65, CH], "TW"); ZAs = sbn([65, CH], "ZAs"); SG = sbn([128, CH], "SG")
        SIGW = sbn([64, H, CH], "SIGW"); CUM = sbn([64, H, CH], "CUM"); CUMP = sbn([64, H, CH], "CUMP")
        Ps = [sbn([64, H, CH], "P%d" % i) for i in range(2)]; IP = sbn([64, H, CH], "IP"); PP = sbn([64, H, CH], "PP")
        A_ = sbn([64, H, CH], "A_"); KK = sbn([64, H, CH], "KK"); SQ = sbn([64, H, CH], "SQ")
        RN = sbn([64, H, CH], "RN"); T1 = sbn([64, H, CH], "T1"); KMs = [sbn([64, H, CH], "KM%d" % i) for i in range(2)]
        B0 = sbn([64, H, CH], "B0")
        RAs = [sbn([64, H, 2, CH], "RA%d" % i) for i in range(2)]
        BTs = [sbn([64, H, CH], "BT%d" % i) for i in range(2)]; KTs = [sbn([64, H, CH], "KT%d" % i) for i in range(2)]
        BHs = [sbn([64, H, CH], "BH%d" % i) for i in range(2)]; KHs = [sbn([64, H, CH], "KH%d" % i) for i in range(2)]
        G1S = sbn([64, H, 3, CH], "G1S")
        G2S = sbn([64, H, 2, CH], "G2S")
        XTa = sbn([64, H, CH], "XTa"); XTb = sbn([64, H, CH], "XTb")
        XMa = sbn([64, H, 2, CH], "XMa"); XMb = sbn([64, H, 2, CH], "XMb")
        VT = sbn([64, H, CH], "VT"); KHT = sbn([64, H, CH], "KHT"); BHT = sbn([64, H, CH], "BHT")
        WT = sbn([64, H, CH], "WT"); UT = sbn([64, H, CH], "UT")
        ST = [sbn([64, H, CH], "ST%d" % i) for i in range(2)]
        STP = sbn([64, H, CH], "STP")
        YS = sbn([64, H, CH], "YS"); YQ = sbn([64, H, CH], "YQ"); MEAN = sbn([64, H, CH], "MEAN")
        MSQ = sbn([64, H, CH], "MSQ"); VAR = sbn([64, H, CH], "VAR"); RK = sbn([64, H, CH], "RK")
        GSs = [sbn([64, H, CH], "GS%d" % i) for i in range(2)]
        YO = [sbn([64, H, CH], "YO%d" % i) for i in range(2)]
        def pbv(i):
            return PB[i][0][0:64, :].rearrange("p (h t) -> p h t", h=H)

        def pb2(i):
            raise NotImplementedError

        op("vector", lambda e: e.memset(ST[0][0][:], 0.0), writes=[ST[0][1]])
        op("vector", lambda e: e.memset(TW[0][64:65, :], 1.0), writes=[TW[1]])
        op("vector", lambda e: e.memset(ZAs[0][64:65, :], 1.0), writes=[ZAs[1]])


        def load(c):
            i = c % NBUF
            t0 = c * CH
            for w in range(3):
                op("sync", lambda e, w=w: e.dma_start(out=Zin[i][0][:, w], in_=rkv_v[:, w, :, t0:t0 + CH + 1]),
                   writes=[Zin[i][1]], dma="zin%d" % i)
            op("sync", lambda e: e.dma_start(out=ZWi[i][0][:], in_=zw[:, t0:t0 + CH + 1]),
               writes=[ZWi[i][1]], dma="zwi%d" % i)
            op("sync", lambda e: e.dma_start(out=ZAi[i][0][:], in_=za[:, t0:t0 + CH + 1]),
               writes=[ZAi[i][1]], dma="zai%d" % i)
            op("sync", lambda e: e.dma_start(out=ZGi[i][0][:], in_=zg[:, t0:t0 + CH + 1]),
               writes=[ZGi[i][1]], dma="zgi%d" % i)

        def TT(eng, out, in0, in1, alu, reads, writes):
            op(eng, lambda e: e.tensor_tensor(out=out, in0=in0, in1=in1, op=alu), reads=reads, writes=writes)

        def ACT(out, in_, func, reads, writes, scale=None, bias=None):
            kw = {}
            if scale is not None:
                kw["scale"] = scale
            if bias is not None:
                kw["bias"] = bias
            op("scalar", lambda e: e.activation(out=out, in_=in_, func=func, **kw), reads=reads, writes=writes)

        def MM(out, lhsT, rhs, reads, writes, start=True, stop=True):
            op("tensor", lambda e: e.matmul(out, lhsT, rhs, start=start, stop=stop), reads=reads, writes=writes)

        def prep(c):
            XS, KM, RA, BT, KT, BH, KH, P, GS = (t[c % 2] for t in (XSs, KMs, RAs, BTs, KTs, BHs, KHs, Ps, GSs))
            i = c % NBUF
            zin, zin_b = Zin[i]
            cur = zin[:, :, :, 1:CH + 1]
            prv = zin[:, :, :, 0:CH]
            TT("gpsimd", Dt[0][:], prv, cur, ALU.subtract, [zin_b], [Dt[1]])
            tm3 = pv_t[:, 0:3, :].unsqueeze(3).to_broadcast([64, 3, H, CH])
            TT("gpsimd", Dt[0][:], Dt[0][:], tm3, ALU.mult, [Dt[1], pv_tb], [Dt[1]])
            TT("vector", XS[0][:], Dt[0][:], cur, ALU.add, [Dt[1], zin_b], [XS[1]])
            xr, xk, xv = XS[0][:, 0], XS[0][:, 1], XS[0][:, 2]
            for (zi, dd, col, npart, dst) in ((ZWi[i], Dw, 0, 64, None), (ZAi[i], Da, 1, 64, ZAs), (ZGi[i], Dg, 2, 128, None)):
                TT("gpsimd", dd[0][:], zi[0][:, 0:CH], zi[0][:, 1:CH + 1], ALU.subtract, [zi[1]], [dd[1]])
                o = dst[0][0:64, :] if dst is not None else dd[0][:]
                ob = dst[1] if dst is not None else dd[1]
                op("vector", lambda e, o=o, dd=dd, col=col, npart=npart, zi=zi: e.scalar_tensor_tensor(
                    out=o, in0=dd[0][:], scalar=psm_t[0:npart, col:col + 1], in1=zi[0][:, 1:CH + 1],
                    op0=ALU.mult, op1=ALU.add), reads=[dd[1], zi[1], psm_tb], writes=[ob])
            ACT(TW[0][0:64, :], Dw[0][:], AF.Tanh, [Dw[1]], [TW[1]])
            ACT(SG[0][:], Dg[0][:], AF.Sigmoid, [Dg[1]], [SG[1]])
            for h in range(H):
                MM(pbv(3)[:, h, :], wup_t[:, h, :], TW[0][:], [wup_tb, TW[1]], [PB[3][1]])
            for h in range(H):
                MM(pbv(6)[:, h, :], aup_t[:, h, :], ZAs[0][:], [aup_tb, ZAs[1]], [PB[6][1]])
            for h in range(H):
                MM(pbv(7)[:, h, :], gup_t[:, h, :], SG[0][:], [gup_tb, SG[1]], [PB[7][1]])
            ACT(SIGW[0][:], pbv(3), AF.Sigmoid, [PB[3][1]], [SIGW[1]])
            ACT(A_[0][:], pbv(6), AF.Sigmoid, [PB[6][1]], [A_[1]])
            ACT(GS[0][:], pbv(7), AF.Copy, [PB[7][1]], [GS[1]])
            op("vector", lambda e: e.tensor_tensor_scan(
                out=CUM[0][:].rearrange("p h t -> p (h t)"), data0=rst_t[:].rearrange("p h t -> p (h t)"),
                data1=SIGW[0][:].rearrange("p h t -> p (h t)"), initial=0.0, op0=ALU.mult, op1=ALU.add),
               reads=[SIGW[1], rst_tb], writes=[CUM[1]])
            TT("gpsimd", CUMP[0][:], CUM[0][:], SIGW[0][:], ALU.subtract, [CUM[1], SIGW[1]], [CUMP[1]])
            ACT(P[0][:], CUM[0][:], AF.Exp, [CUM[1]], [P[1]], scale=-C0)
            ACT(IP[0][:], CUM[0][:], AF.Exp, [CUM[1]], [IP[1]], scale=C0)
            ACT(PP[0][:], CUMP[0][:], AF.Exp, [CUMP[1]], [PP[1]], scale=-C0)
            TT("gpsimd", KK[0][:], xk, bc_t(pv_t[:, 3, :]), ALU.mult, [XS[1], pv_tb], [KK[1]])
            TT("gpsimd", SQ[0][:], KK[0][:], KK[0][:], ALU.mult, [KK[1]], [SQ[1]])
            MM(PB[3][0][0:64, :], ones, SQ[0][:].rearrange("p h t -> p (h t)"), [cst_tb, SQ[1]], [PB[3][1]])
            op("vector", lambda e: e.tensor_scalar(out=RN[0][:], in0=pbv(3), scalar1=1e-24, scalar2=None, op0=ALU.max),
               reads=[PB[3][1]], writes=[RN[1]])
            ACT(RN[0][:], RN[0][:], AF.Sqrt, [RN[1]], [RN[1]])
            op("vector", lambda e: e.reciprocal(out=RN[0][:], in_=RN[0][:]), reads=[RN[1]], writes=[RN[1]])
            TT("vector", KK[0][:], KK[0][:], RN[0][:], ALU.mult, [KK[1], RN[1]], [KK[1]])
            op("vector", lambda e: e.scalar_tensor_tensor(out=T1[0][:], in0=A_[0][:], scalar=-1.0, in1=bc_t(pv_t[:, 4, :]),
                                                          op0=ALU.add, op1=ALU.mult), reads=[A_[1], pv_tb], writes=[T1[1]])
            op("vector", lambda e: e.scalar_tensor_tensor(out=KM[0][:], in0=T1[0][:], scalar=1.0, in1=xk,
                                                          op0=ALU.add, op1=ALU.mult), reads=[T1[1], XS[1]], writes=[KM[1]])
            TT("vector", B0[0][:], KK[0][:], A_[0][:], ALU.mult, [KK[1], A_[1]], [B0[1]])
            TT("vector", RA[0][:, :, 0, :], xr, P[0][:], ALU.mult, [XS[1], P[1]], [RA[1]])
            op("vector", lambda e: e.scalar_tensor_tensor(out=RA[0][:, :, 1, :], in0=KK[0][:], scalar=-1.0, in1=PP[0][:],
                                                          op0=ALU.mult, op1=ALU.mult), reads=[KK[1], PP[1]], writes=[RA[1]])
            TT("gpsimd", BT[0][:], B0[0][:], IP[0][:], ALU.mult, [B0[1], IP[1]], [BT[1]])
            TT("gpsimd", KT[0][:], KM[0][:], IP[0][:], ALU.mult, [KM[1], IP[1]], [KT[1]])
            pc_b = P[0][:, :, CH - 1:CH].to_broadcast([64, H, CH])
            TT("gpsimd", BH[0][:], BT[0][:], pc_b, ALU.mult, [BT[1], P[1]], [BH[1]])
            TT("gpsimd", KH[0][:], KT[0][:], pc_b, ALU.mult, [KT[1], P[1]], [KH[1]])

        def body(c, thunks):
            XS, KM, RA, BT, KT, BH, KH, P, GS = (t[c % 2] for t in (XSs, KMs, RAs, BTs, KTs, BHs, KHs, Ps, GSs))
            i = c % NBUF
            xr, xk, xv = XS[0][:, 0], XS[0][:, 1], XS[0][:, 2]
            pc_b = P[0][:, :, CH - 1:CH].to_broadcast([64, H, CH])
            def g2v(b0):
                return [PB[b0 + hh // 4][0][0:64, (hh % 4) * 128:(hh % 4) * 128 + 128] for hh in range(H)]
            g1 = g2v(4)
            g2 = g2v(6)
            for h in range(H):
                ra_h = RA[0][:, h, :, :].rearrange("p a t -> p (a t)")
                MM(g1[h], BT[0][:, h, :], ra_h, [BT[1], RA[1]], [PB[4 + h // 4][1]])
            for h in range(H):
                ra_h = RA[0][:, h, :, :].rearrange("p a t -> p (a t)")
                MM(g2[h], KT[0][:, h, :], ra_h, [KT[1], RA[1]], [PB[6 + h // 4][1]])
            for h in range(H):
                MM(pbv(0)[:, h, :], RA[0][:, h, 1, :], BT[0][:, h, :], [BT[1], RA[1]], [PB[0][1]])
            for half in range(2):
                hs = slice(half * 4, half * 4 + 4)
                src = PB[4 + half][0][0:64, :].rearrange("p (h a t) -> p h a t", h=4, a=2)
                msk = msw_t[:].unsqueeze(1).to_broadcast([64, 4, 2, CH])
                TT("vector", G1S[0][:, hs, 0:2, :], src, msk, ALU.mult, [PB[4 + half][1], msw_tb], [G1S[1]])
                src2 = PB[6 + half][0][0:64, :].rearrange("p (h a t) -> p h a t", h=4, a=2)
                TT("vector", G2S[0][:, hs, :, :], src2, msk, ALU.mult, [PB[6 + half][1], msw_tb], [G2S[1]])
            TT("vector", XTa[0][:], pbv(0), bc_h(m_sl), ALU.mult, [PB[0][1], cst_tb], [XTa[1]])
            TT("gpsimd", G1S[0][:, :, 2, :], G1S[0][:, :, 1, :], bc_h(ident), ALU.add, [G1S[1], cst_tb], [G1S[1]])
            Xp = (G1S, lambda h: G1S[0][:, h, 1, :], lambda h: G1S[0][:, h, 1:3, :].rearrange("p a t -> p (a t)"))
            XTp = XTa
            XTn = XTb
            XMn, XMo = XMa, XMb
            for h in range(H):
                MM(pbv(1)[:, h, :], XTp[0][:, h, :], Xp[1](h), [XTp[1], Xp[0][1]], [PB[1][1]])
            for h in range(H):
                MM(pbv(2)[:, h, :], Xp[1](h), XTp[0][:, h, :], [XTp[1], Xp[0][1]], [PB[2][1]])
            ACT(XMn[0][:, :, 0, :], pbv(1), AF.Copy, [PB[1][1]], [XMn[1]])
            op("gpsimd", lambda e, XMn=XMn: e.tensor_copy(out=XMn[0][:, :, 1, :], in_=G1S[0][:, :, 2, :]), reads=[G1S[1]], writes=[XMn[1]])
            ACT(XTn[0][:], pbv(2), AF.Copy, [PB[2][1]], [XTn[1]])
            XMc, XTc = XMn, XTn
            XMn = XMo
            XTn = XTa
            nth = len(thunks)
            per = (nth + 5) // 6
            S.flush(thunks[0:per])
            for lvl in range(2, 7):
                last = lvl == 6
                pa = g2v(4)
                for h in range(H):
                    if last:
                        MM(pa[h][:, 64:128], XTc[0][:, h, :], XMc[0][:, h, 1, :], [XTc[1], XMc[1]], [PB[4 + h // 4][1]])
                    else:
                        MM(pa[h], XTc[0][:, h, :], XMc[0][:, h, :, :].rearrange("p a t -> p (a t)"),
                           [XTc[1], XMc[1]], [PB[4 + h // 4][1]])
                if not last:
                    for h in range(H):
                        MM(pbv(0)[:, h, :], XMc[0][:, h, 0, :], XTc[0][:, h, :], [XTc[1], XMc[1]], [PB[0][1]])
                for half in range(2):
                    hs = slice(half * 4, half * 4 + 4)
                    src = PB[4 + half][0][0:64, :].rearrange("p (h a t) -> p h a t", h=4, a=2)
                    if not last:
                        ACT(XMn[0][:, hs, 0, :], src[:, :, 0, :], AF.Copy, [PB[4 + half][1]], [XMn[1]])
                    TT("vector", XMn[0][:, hs, 1, :], src[:, :, 1, :], XMc[0][:, hs, 1, :], ALU.add,
                       [PB[4 + half][1], XMc[1]], [XMn[1]])
                if not last:
                    ACT(XTn[0][:], pbv(0), AF.Copy, [PB[0][1]], [XTn[1]])
                XMc, XMn = XMn, XMc
                XTc, XTn = XTn, XTc
                S.flush(thunks[(lvl - 1) * per:lvl * per])
            Mfin = XMc
            for (src_ap, srcb, bank, dst) in ((lambda h: xv[:, h, :], XS[1], 1, VT), (lambda h: KH[0][:, h, :], KH[1], 2, KHT),
                                              (lambda h: BH[0][:, h, :], BH[1], 3, BHT)):
                for h in range(H):
                    op("tensor", lambda e, h=h, src_ap=src_ap, bank=bank: e.transpose(pbv(bank)[:, h, :], src_ap(h), ident),
                       reads=[srcb, cst_tb], writes=[PB[bank][1]])
                ACT(dst[0][:], pbv(bank), AF.Copy, [PB[bank][1]], [dst[1]])
            Sc, Sn = ST[c % 2], ST[(c + 1) % 2]
            TT("gpsimd", STP[0][:], Sc[0][:], pc_b, ALU.mult, [Sc[1], P[1]], [STP[1]])
            for h in range(H):
                MM(pbv(6)[:, h, :], G2S[0][:, h, 1, :], VT[0][:, h, :], [G2S[1], VT[1]], [PB[6][1]], start=True, stop=False)
                MM(pbv(6)[:, h, :], RA[0][:, h, 1, :], Sc[0][:, h, :], [RA[1], Sc[1]], [PB[6][1]], start=False, stop=True)
            ACT(WT[0][:], pbv(6), AF.Copy, [PB[6][1]], [WT[1]])
            for h in range(H):
                MM(pbv(7)[:, h, :], Mfin[0][:, h, 1, :], WT[0][:, h, :], [Mfin[1], WT[1]], [PB[7][1]])
            ACT(UT[0][:], pbv(7), AF.Copy, [PB[7][1]], [UT[1]])
            for h in range(H):
                MM(pbv(6)[:, h, :], BHT[0][:, h, :], UT[0][:, h, :], [BHT[1], UT[1]], [PB[6][1]], start=True, stop=False)
                MM(pbv(6)[:, h, :], KHT[0][:, h, :], VT[0][:, h, :], [KHT[1], VT[1]], [PB[6][1]], start=False, stop=True)
            for h in range(H):
                MM(pbv(7)[:, h, :], Sc[0][:, h, :], RA[0][:, h, 0, :], [Sc[1], RA[1]], [PB[7][1]], start=True, stop=False)
                MM(pbv(7)[:, h, :], UT[0][:, h, :], G1S[0][:, h, 0, :], [UT[1], G1S[1]], [PB[7][1]], start=False, stop=False)
                MM(pbv(7)[:, h, :], VT[0][:, h, :], G2S[0][:, h, 0, :], [VT[1], G2S[1]], [PB[7][1]], start=False, stop=True)
            TT("vector", Sn[0][:], STP[0][:], pbv(6), ALU.add, [STP[1], PB[6][1]], [Sn[1]])
            ACT(YS[0][:], pbv(7), AF.Copy, [PB[7][1]], [YS[1]])
            TT("gpsimd", YQ[0][:], YS[0][:], YS[0][:], ALU.mult, [YS[1]], [YQ[1]])
            TT("gpsimd", RK[0][:], xr, KM[0][:], ALU.mult, [XS[1], KM[1]], [RK[1]])
            TT("gpsimd", RK[0][:], RK[0][:], bc_t(pv_t[:, 6, :]), ALU.mult, [RK[1], pv_tb], [RK[1]])
            MM(PB[1][0][0:64, :], onesm_t[:], YS[0][:].rearrange("p h t -> p (h t)"), [onesm_tb, YS[1]], [PB[1][1]])
            MM(PB[2][0][0:64, :], onesm_t[:], YQ[0][:].rearrange("p h t -> p (h t)"), [onesm_tb, YQ[1]], [PB[2][1]])
            MM(PB[3][0][0:64, :], ones, RK[0][:].rearrange("p h t -> p (h t)"), [cst_tb, RK[1]], [PB[3][1]])
            ACT(MEAN[0][:], pbv(1), AF.Copy, [PB[1][1]], [MEAN[1]])
            TT("gpsimd", MSQ[0][:], MEAN[0][:], MEAN[0][:], ALU.mult, [MEAN[1]], [MSQ[1]])
            TT("vector", VAR[0][:], pbv(2), MSQ[0][:], ALU.subtract, [PB[2][1], MSQ[1]], [VAR[1]])
            op("vector", lambda e: e.tensor_scalar(out=VAR[0][:], in0=VAR[0][:], scalar1=GN_EPS, scalar2=None, op0=ALU.add),
               reads=[VAR[1]], writes=[VAR[1]])
            ACT(VAR[0][:], VAR[0][:], AF.Sqrt, [VAR[1]], [VAR[1]])
            op("vector", lambda e: e.reciprocal(out=VAR[0][:], in_=VAR[0][:]), reads=[VAR[1]], writes=[VAR[1]])
            TT("gpsimd", YS[0][:], YS[0][:], MEAN[0][:], ALU.subtract, [YS[1], MEAN[1]], [YS[1]])
            TT("vector", YS[0][:], YS[0][:], VAR[0][:], ALU.mult, [YS[1], VAR[1]], [YS[1]])
            TT("gpsimd", YS[0][:], YS[0][:], bc_t(pv_t[:, 7, :]), ALU.mult, [YS[1], pv_tb], [YS[1]])
            TT("gpsimd", YS[0][:], YS[0][:], bc_t(pv_t[:, 8, :]), ALU.add, [YS[1], pv_tb], [YS[1]])
            TT("vector", RK[0][:], pbv(3), xv, ALU.mult, [PB[3][1], XS[1]], [RK[1]])
            TT("gpsimd", YS[0][:], YS[0][:], RK[0][:], ALU.add, [YS[1], RK[1]], [YS[1]])
            yo_t, yo_tb = YO[c % 2]
            TT("vector", yo_t[:], YS[0][:], GS[0][:], ALU.mult, [YS[1], GS[1]], [yo_tb])
            t0 = c * CH
            op("sync", lambda e, yo_t=yo_t, t0=t0: e.dma_start(out=yo_v[:, :, t0:t0 + CH], in_=yo_t[:]),
               reads=[yo_tb], dma="yo%d" % (c % 2))

        load(0)
        if nchunks > 1:
            load(1)
        prep(0)
        for c in range(nchunks):
            if c + 2 < nchunks:
                load(c + 2)
            thunks = []
            if c + 1 < nchunks:
                S.defer = thunks
                prep(c + 1)
                S.defer = None
            body(c, thunks)


def b_consts():
    s = np.arange(CH)[:, None]
    t = np.arange(CH)[None, :]
    cst = np.zeros((64, 6, 64), np.float32)
    cst[:, 0] = (s < t)
    cst[:, 1] = (s <= t)
    cst[:, 2] = (s > t)
    cst[:, 3] = np.eye(64)
    cst[:, 4] = 1.0
    return cst


def b_inputs(zr_b, p, l, hh, nchunks):
    TP1 = nchunks * CH + 1
    Tb = zr_b.shape[1]
    hs = slice(hh * 8, hh * 8 + 8)
    rkv = np.zeros((3, 8, 64, TP1), np.float32)
    for w in range(3):
        rkv[w, :, :, 1:1 + Tb] = zr_b[w * D:(w + 1) * D].reshape(16, 64, Tb)[hs]
    zw = np.zeros((64, TP1), np.float32); zw[:, 1:1 + Tb] = zr_b[3 * D:3 * D + 64]
    za = np.zeros((64, TP1), np.float32); za[:, 1:1 + Tb] = zr_b[3 * D + 64:3 * D + 128]
    zg = np.zeros((128, TP1), np.float32); zg[:, 1:1 + Tb] = zr_b[3 * D + 128:3 * D + 256]
    tm = p["time_mix"][l]
    hv = lambda v: np.ascontiguousarray(v.reshape(16, 64)[hs].T)
    pv = np.zeros((64, 11, 8), np.float32)
    pv[:, 0] = hv(tm[0:D]); pv[:, 1] = hv(tm[D:2 * D]); pv[:, 2] = hv(tm[2 * D:3 * D])
    pv[:, 3] = hv(p["k_k"][l]); pv[:, 4] = hv(p["k_a"][l])
    pv[:, 6] = hv(p["r_k"][l].reshape(-1)); pv[:, 7] = hv(p["lnx_g"][l]); pv[:, 8] = hv(p["lnx_b"][l])
    psm = np.zeros((128, 4), np.float32)
    psm[0:64, 0] = tm[3 * D:3 * D + 64]; psm[0:64, 1] = tm[3 * D + 64:3 * D + 128]; psm[:, 2] = tm[3 * D + 128:3 * D + 256]
    wup = np.zeros((65, 8, 64), np.float32)
    wup[0:64] = p["w_up"][l].reshape(64, 16, 64)[:, hs]; wup[64] = p["w0"][l].reshape(16, 64)[hs]
    aup = np.zeros((65, 8, 64), np.float32)
    aup[0:64] = p["a_up"][l].reshape(64, 16, 64)[:, hs]; aup[64] = p["a0"][l].reshape(16, 64)[hs]
    gup = np.ascontiguousarray(p["g_up"][l].reshape(128, 16, 64)[:, hs])
    return {"rkv": rkv, "zw": zw, "za": za, "zg": zg, "pv": pv, "psm": psm, "wup": wup, "aup": aup, "gup": gup,
            "cst": b_consts()}


TOK = T // 2
HALO = 56
WIN = TOK + HALO
NBK = 416
NBLK = WIN // NBK


class TP:
    def __init__(self, cx, banks, n, slabmode=False):
        self.slabmode = slabmode
        self.q_w = "sync" if slabmode else "gpsimd"
        self.q_a = "gpsimd" if slabmode else "sync"
        self.cx = cx
        self.S = self.cx.S
        self.op = self.S.op
        self.n = n
        self.banks = banks
        self.bi = 0
        self.slabs = [cx.sb([128, 22, 128], BF16, name="slab%d" % i) for i in range(8)]
        self.si = 0
        self.stg = [cx.sb([128, n], F32, name="stg%d" % i) for i in range(4)]
        self.gi = 0
        self.onesD = cx.sb([128, 128], F32, name="onesD", const=True)
        self.op("vector", lambda e: e.memset(self.onesD[0][:], 1.0 / D), writes=[self.onesD[1]])
        self.mean = cx.sb([128, n], F32, name="ln_mean")
        self.msq = cx.sb([128, n], F32, name="ln_msq")
        self.rstd = cx.sb([128, n], F32, name="ln_rstd")
        self.tmp = [cx.sb([128, n], F32, name="tmp%d" % i) for i in range(2)]
        self.ti = 0

    def bank(self):
        b = self.banks[self.bi % 8]
        self.bi += 1
        return b

    def next_tmp(self):
        t = self.tmp[self.ti % 2]
        self.ti += 1
        return t

    def MM(self, out, lhsT, rhs, reads, writes, start, stop):
        self.op("tensor", lambda e: e.matmul(out, lhsT, rhs, start=start, stop=stop), reads=reads, writes=writes)

    def linear(self, X, KC, groups, evac):
        n = self.n
        for gi, grp in enumerate(groups):
            bks = []
            for (wv, col0) in grp:
                k = self.si % 8
                self.si += 1
                slab, slab_b = self.slabs[k]
                if self.slabmode:
                    self.op(self.q_w, lambda e, slab=slab, wv=wv, col0=col0: e.dma_start(
                        out=slab[:, 0:KC, :], in_=wv[col0 // 128]), writes=[slab_b], dma="slab%d" % k)
                else:
                    self.op(self.q_w, lambda e, slab=slab, wv=wv, col0=col0: e.dma_start(
                        out=slab[:, 0:KC, :], in_=wv[:, :, col0:col0 + 128]), writes=[slab_b], dma="slab%d" % k)
                bk = self.bank()
                for kc in range(KC):
                    self.MM(bk[0][:, 0:n], slab[:, kc, :], X[0][:, kc, :], [slab_b, X[1]], [bk[1]], kc == 0, kc == KC - 1)
                bks.append(bk)
            evac(gi, bks)

    def store(self, dram_ap, src_tile_fn):
        k = self.gi % 4
        self.gi += 1
        st, st_b = self.stg[k]
        src_tile_fn(st, st_b)
        self.op(self.q_a, lambda e: e.dma_start(out=dram_ap, in_=st[:]), reads=[st_b], dma="stg%d" % k)

    def layernorm(self, X, g_ap, b_ap, SQ, out_bf, silu=False):
        op = self.op
        n = self.n
        Xt, Xb = X
        SQt, SQb = SQ
        op("gpsimd", lambda e: e.tensor_tensor(out=SQt[:], in0=Xt[:], in1=Xt[:], op=ALU.mult), reads=[Xb], writes=[SQb])
        bm = self.bank()
        bq = self.bank()
        for dc in range(8):
            self.MM(bm[0][:, 0:n], self.onesD[0][:], Xt[:, dc, :], [self.onesD[1], Xb], [bm[1]], dc == 0, dc == 7)
        for dc in range(8):
            self.MM(bq[0][:, 0:n], self.onesD[0][:], SQt[:, dc, :], [self.onesD[1], SQb], [bq[1]], dc == 0, dc == 7)
        mean, msq, rstd = self.mean, self.msq, self.rstd
        op("scalar", lambda e: e.activation(out=mean[0][:], in_=bm[0][:, 0:n], func=AF.Copy), reads=[bm[1]], writes=[mean[1]])
        op("gpsimd", lambda e: e.tensor_tensor(out=msq[0][:], in0=mean[0][:], in1=mean[0][:], op=ALU.mult),
           reads=[mean[1]], writes=[msq[1]])
        op("vector", lambda e: e.scalar_tensor_tensor(out=rstd[0][:], in0=bq[0][:, 0:n], scalar=LN_EPS, in1=msq[0][:],
                                                      op0=ALU.add, op1=ALU.subtract), reads=[bq[1], msq[1]], writes=[rstd[1]])
        op("scalar", lambda e: e.activation(out=rstd[0][:], in_=rstd[0][:], func=AF.Sqrt), reads=[rstd[1]], writes=[rstd[1]])
        op("vector", lambda e: e.reciprocal(out=rstd[0][:], in_=rstd[0][:]), reads=[rstd[1]], writes=[rstd[1]])
        bc = lambda t: t[0][:].unsqueeze(1).to_broadcast([128, 8, n])
        pb = lambda a: a.unsqueeze(2).to_broadcast([128, 8, n])
        op("gpsimd", lambda e: e.tensor_tensor(out=Xt[:], in0=Xt[:], in1=bc(mean), op=ALU.subtract), reads=[Xb, mean[1]], writes=[Xb])
        op("vector", lambda e: e.tensor_tensor(out=Xt[:], in0=Xt[:], in1=bc(rstd), op=ALU.mult), reads=[Xb, rstd[1]], writes=[Xb])
        op("gpsimd", lambda e: e.tensor_tensor(out=Xt[:], in0=Xt[:], in1=pb(g_ap), op=ALU.mult), reads=[Xb, self.lnp[1]], writes=[Xb])
        op("vector", lambda e: e.tensor_tensor(out=Xt[:], in0=Xt[:], in1=pb(b_ap), op=ALU.add), reads=[Xb, self.lnp[1]], writes=[Xb])
        if out_bf is not None:
            f = AF.Silu if silu else AF.Copy
            op("scalar", lambda e: e.activation(out=out_bf[0][:], in_=Xt[:], func=f), reads=[Xb], writes=[out_bf[1]])

    def ffn(self, Xb16, ACTt, wg_v, wu_v, wd_v, Hres, Xout):
        op = self.op
        n = self.n

        def ev1(fc, bks):
            tmp = self.next_tmp()
            op("scalar", lambda e: e.activation(out=tmp[0][:], in_=bks[0][0][:, 0:n], func=AF.Silu),
               reads=[bks[0][1]], writes=[tmp[1]])
            op("vector", lambda e: e.tensor_tensor(out=ACTt[0][:, fc, :], in0=tmp[0][:], in1=bks[1][0][:, 0:n], op=ALU.mult),
               reads=[tmp[1], bks[1][1]], writes=[ACTt[1]])

        self.linear(Xb16, 8, [[(wg_v, fc * 128), (wu_v, fc * 128)] for fc in range(DFF // 128)], ev1)

        def ev2(dc, bks):
            tmp = self.next_tmp()
            op("scalar", lambda e: e.activation(out=tmp[0][:], in_=bks[0][0][:, 0:n], func=AF.Copy, scale=0.5),
               reads=[bks[0][1]], writes=[tmp[1]])
            op("vector", lambda e: e.scalar_tensor_tensor(out=Xout[0][:, dc, :], in0=Hres[0][:, dc, :], scalar=float(ALPHA),
                                                          in1=tmp[0][:], op0=ALU.mult, op1=ALU.add),
               reads=[Hres[1], tmp[1]], writes=[Xout[1]])

        self.linear(ACTt, DFF // 128, [[(wd_v, dc * 128)] for dc in range(8)], ev2)

    def finish(self):
        S = self.S
        for k in S.dsem:
            S._wait("sync", (S.dsem[k], S.dcnt[k]))
        S.emit()


def wview(ap):
    return ap.rearrange("(kc p) o -> p kc o", p=128)


def aview(ap):
    return ap.rearrange("(c p) t -> p c t", p=128)


def build_A(nblk=NBLK, n=NBK):
    W = nblk * n
    nc = bass.Bass("TRN2", target_bir_lowering=False)
    stack = ExitStack()
    with stack:
        cx = Ctx(nc, stack)
        banks = [cx.ps([128, 512], name="bk%d" % i) for i in range(8)]
        tp = TP(cx, banks, n)
        dr = lambda name, shape, kind="ExternalInput": nc.dram_tensor(name, list(shape), F32, kind=kind).ap()
        d = {"hT": dr("hT", [D, W]), "wg": dr("wg", [D, DFF]), "wu": dr("wu", [D, DFF]), "wd": dr("wd", [DFF, D]),
             "w_in": dr("w_in", [D, NIN]), "cwo": dr("cwo", [D, D]), "convw": dr("convw", [D, CW]),
             "lnp": dr("lnp", [128, 5, 8]), "flag": dr("flag", [128, 1]),
             "h1": dr("h1", [D, W], "ExternalOutput"), "zr": dr("zr", [RWKV_COLS, W], "ExternalOutput"),
             "gr": dr("gr", [D, W], "ExternalOutput"), "cp": dr("cp", [D, W], "ExternalOutput")}
        emit_A(tp, d, nblk, n)
        tp.finish()
    return nc


def emit_A(tp, d, nblk, n, use_flag=True):
    if True:
        cx, op = tp.cx, tp.op
        hT_in = aview(d["hT"])
        wv_ = (lambda a: a) if tp.slabmode else wview
        wg_v = wv_(d["wg"]); wu_v = wv_(d["wu"]); wd_v = wv_(d["wd"])
        win_v = wv_(d["w_in"]); cwo_v = wv_(d["cwo"])
        convw = d["convw"]; lnp_d = d["lnp"]
        h1_out = aview(d["h1"])
        zr_out = d["zr"]; gr_out = d["gr"]; cp_out = d["cp"]

        lnp = cx.sb([128, 5, 8], name="lnp", const=True); tp.lnp = lnp
        cw = cx.sb([128, 8, CW], name="cw", const=True)
        op("sync", lambda e: e.dma_start(out=lnp[0][:], in_=lnp_d), writes=[lnp[1]], dma="c0")
        op("sync", lambda e: e.dma_start(out=cw[0][:], in_=convw.rearrange("(c p) j -> p c j", p=128)), writes=[cw[1]], dma="c1")
        if use_flag:
            flag = cx.sb([128, 1], name="flag", const=True)
            op("sync", lambda e: e.dma_start(out=flag[0][:], in_=d["flag"]), writes=[flag[1]], dma="c2")

        R1 = cx.sb([128, 8, n], name="R1"); R2 = cx.sb([128, 8, n], name="R2")
        hTb = cx.sb([128, 8, n], BF16, name="hTb"); h1b = cx.sb([128, 8, n], BF16, name="h1b")
        ACTt = cx.sb([128, DFF // 128, n], BF16, name="ACTt")
        U = cx.sb([128, 8, CW - 1 + n], name="U")
        GC = cx.sb([128, 8, n], name="GC")
        UC = cx.sb([128, 8, n], BF16, name="UC")
        op("vector", lambda e: e.memset(U[0][:], 0.0), writes=[U[1]])

        for blk in range(nblk):
            t0 = blk * n
            op(tp.q_a, lambda e, t0=t0: e.dma_start(out=R1[0][:], in_=hT_in[:, :, t0:t0 + n]), writes=[R1[1]], dma="ldh")
            op("scalar", lambda e: e.activation(out=hTb[0][:], in_=R1[0][:], func=AF.Copy), reads=[R1[1]], writes=[hTb[1]])
            tp.ffn(hTb, ACTt, wg_v, wu_v, wd_v, R1, R2)
            tp.layernorm(R2, lnp[0][:, 0, :], lnp[0][:, 1, :], R1, h1b)
            op(tp.q_a, lambda e, t0=t0: e.dma_start(out=h1_out[:, :, t0:t0 + n], in_=R2[0][:]), reads=[R2[1]], dma="sth1")

            def ev_glu(c, bks):
                tmp = tp.next_tmp()
                op("scalar", lambda e: e.activation(out=tmp[0][:], in_=bks[1][0][:, 0:n], func=AF.Sigmoid),
                   reads=[bks[1][1]], writes=[tmp[1]])
                op("vector", lambda e: e.tensor_tensor(out=U[0][:, c, CW - 1:CW - 1 + n], in0=tmp[0][:], in1=bks[0][0][:, 0:n],
                                                       op=ALU.mult), reads=[tmp[1], bks[0][1]], writes=[U[1]])
            tp.linear(h1b, 8, [[(win_v, c * 128), (win_v, D + c * 128)] for c in range(8)], ev_glu)

            def ev_zr(j, bks, t0=t0):
                tp.store(zr_out[j * 128:(j + 1) * 128, t0:t0 + n],
                         lambda st, st_b: op("scalar", lambda e: e.activation(out=st[:], in_=bks[0][0][:, 0:n], func=AF.Copy),
                                             reads=[bks[0][1]], writes=[st_b]))
            tp.linear(h1b, 8, [[(win_v, 2 * D + j * 128)] for j in range(RWKV_COLS // 128)], ev_zr)

            def ev_gc(c, bks):
                op("scalar", lambda e: e.activation(out=GC[0][:, c, :], in_=bks[0][0][:, 0:n], func=AF.Sigmoid),
                   reads=[bks[0][1]], writes=[GC[1]])
            tp.linear(h1b, 8, [[(win_v, 2 * D + RWKV_COLS + c * 128)] for c in range(8)], ev_gc)

            def ev_gr(c, bks, t0=t0):
                tp.store(gr_out[c * 128:(c + 1) * 128, t0:t0 + n],
                         lambda st, st_b: op("scalar", lambda e: e.activation(out=st[:], in_=bks[0][0][:, 0:n], func=AF.Sigmoid),
                                             reads=[bks[0][1]], writes=[st_b]))
            tp.linear(h1b, 8, [[(win_v, 3 * D + RWKV_COLS + c * 128)] for c in range(8)], ev_gr)

            if blk == 0 and use_flag:
                op("vector", lambda e: e.tensor_scalar(out=U[0][:, :, CW - 1:CW - 1 + HALO], in0=U[0][:, :, CW - 1:CW - 1 + HALO],
                                                       scalar1=flag[0][:, 0:1], scalar2=None, op0=ALU.mult),
                   reads=[U[1], flag[1]], writes=[U[1]])
            for c in range(8):
                op("vector", lambda e, c=c: e.tensor_scalar(out=R1[0][:, c, :], in0=U[0][:, c, 0:n], scalar1=cw[0][:, c, 0:1],
                                                            scalar2=lnp[0][:, 2, c:c + 1], op0=ALU.mult, op1=ALU.add),
                   reads=[U[1], cw[1], lnp[1]], writes=[R1[1]])
                for j in range(1, CW):
                    op("vector", lambda e, c=c, j=j: e.scalar_tensor_tensor(
                        out=R1[0][:, c, :], in0=U[0][:, c, j:j + n], scalar=cw[0][:, c, j:j + 1], in1=R1[0][:, c, :],
                        op0=ALU.mult, op1=ALU.add), reads=[U[1], cw[1], R1[1]], writes=[R1[1]])
            op("scalar", lambda e: e.activation(out=U[0][:, :, 0:CW - 1], in_=U[0][:, :, n:n + CW - 1], func=AF.Copy),
               reads=[U[1]], writes=[U[1]])
            tp.layernorm(R1, lnp[0][:, 3, :], lnp[0][:, 4, :], R2, UC, silu=True)

            def ev_co(dc, bks, t0=t0):
                tp.store(cp_out[dc * 128:(dc + 1) * 128, t0:t0 + n],
                         lambda st, st_b: op("vector", lambda e: e.tensor_tensor(out=st[:], in0=GC[0][:, dc, :], in1=bks[0][0][:, 0:n],
                                                                                 op=ALU.mult),
                                             reads=[GC[1], bks[0][1]], writes=[st_b]))
            tp.linear(UC, 8, [[(cwo_v, dc * 128)] for dc in range(8)], ev_co)


def build_C(nblk=NBLK, n=NBK):
    W = nblk * n
    nc = bass.Bass("TRN2", target_bir_lowering=False)
    stack = ExitStack()
    with stack:
        cx = Ctx(nc, stack)
        banks = [cx.ps([128, 512], name="bk%d" % i) for i in range(8)]
        tp = TP(cx, banks, n)
        dr = lambda name, shape, kind="ExternalInput": nc.dram_tensor(name, list(shape), F32, kind=kind).ap()
        d = {"yB": dr("yB", [D, W]), "h1": dr("h1", [D, W]), "gr": dr("gr", [D, W]), "cp": dr("cp", [D, W]),
             "rwo": dr("rwo", [D, D]), "wout": dr("wout", [D, D]), "wg": dr("wg", [D, DFF]), "wu": dr("wu", [D, DFF]),
             "wd": dr("wd", [DFF, D]), "lnp": dr("lnp", [128, 4, 8]), "hout": dr("hout", [D, W], "ExternalOutput")}
        emit_C(tp, d, nblk, n)
        tp.finish()
    return nc


def emit_C(tp, d, nblk, n):
    if True:
        cx, op = tp.cx, tp.op
        yB_in = aview(d["yB"]); h1_in = aview(d["h1"]); gr_in = aview(d["gr"]); cp_in = aview(d["cp"])
        wv_ = (lambda a: a) if tp.slabmode else wview
        rwo_v = wv_(d["rwo"]); wout_v = wv_(d["wout"])
        wg_v = wv_(d["wg"]); wu_v = wv_(d["wu"]); wd_v = wv_(d["wd"])
        lnp_d = d["lnp"]
        h_out = aview(d["hout"])
        lnp = cx.sb([128, 4, 8], name="lnp", const=True); tp.lnp = lnp
        op("sync", lambda e: e.dma_start(out=lnp[0][:], in_=lnp_d), writes=[lnp[1]], dma="c0")
        R1 = cx.sb([128, 8, n], name="R1"); R2 = cx.sb([128, 8, n], name="R2")
        R3 = cx.sb([128, 8, n], name="R3"); R4 = cx.sb([128, 8, n], name="R4")
        YB = cx.sb([128, 8, n], BF16, name="YB"); HM = cx.sb([128, 8, n], BF16, name="HM")
        h2b = cx.sb([128, 8, n], BF16, name="h2b")
        ACTt = cx.sb([128, DFF // 128, n], BF16, name="ACTt")
        for blk in range(nblk):
            t0 = blk * n
            sl = slice(t0, t0 + n)
            op("gpsimd", lambda e, sl=sl: e.dma_start(out=YB[0][:], in_=yB_in[:, :, sl]), writes=[YB[1]], dma="ldy")
            op(tp.q_a, lambda e, sl=sl: e.dma_start(out=R1[0][:], in_=h1_in[:, :, sl]), writes=[R1[1]], dma="ld1")
            op(tp.q_a, lambda e, sl=sl: e.dma_start(out=R3[0][:], in_=gr_in[:, :, sl]), writes=[R3[1]], dma="ld3")
            op(tp.q_a, lambda e, sl=sl: e.dma_start(out=R4[0][:], in_=cp_in[:, :, sl]), writes=[R4[1]], dma="ld4")

            def ev_r(dc, bks):
                tmp = tp.next_tmp()
                op("vector", lambda e: e.tensor_tensor(out=tmp[0][:], in0=R3[0][:, dc, :], in1=bks[0][0][:, 0:n], op=ALU.mult),
                   reads=[R3[1], bks[0][1]], writes=[tmp[1]])
                op(HM_ENG, lambda e: e.tensor_tensor(out=HM[0][:, dc, :], in0=tmp[0][:], in1=R4[0][:, dc, :], op=ALU.add),
                   reads=[tmp[1], R4[1]], writes=[HM[1]])
            tp.linear(YB, 8, [[(rwo_v, dc * 128)] for dc in range(8)], ev_r)

            def ev_m(dc, bks):
                op("vector", lambda e: e.scalar_tensor_tensor(out=R2[0][:, dc, :], in0=R1[0][:, dc, :], scalar=float(ALPHA),
                                                              in1=bks[0][0][:, 0:n], op0=ALU.mult, op1=ALU.add),
                   reads=[R1[1], bks[0][1]], writes=[R2[1]])
            if C_STAGE >= 2:
                tp.linear(HM, 8, [[(wout_v, dc * 128)] for dc in range(8)], ev_m)
                tp.layernorm(R2, lnp[0][:, 0, :], lnp[0][:, 1, :], R3, h2b)
            if C_STAGE >= 3:
                tp.ffn(h2b, ACTt, wg_v, wu_v, wd_v, R2, R4)
                tp.layernorm(R4, lnp[0][:, 2, :], lnp[0][:, 3, :], R3, None)
            op(tp.q_a, lambda e, sl=sl: e.dma_start(out=h_out[:, :, sl], in_=R4[0][:]), reads=[R4[1]], dma="sth")


def chunkvec(v):
    return np.ascontiguousarray(v.reshape(8, 128).T)


def a_inputs(hT_win, p, l, flag):
    lnp = np.stack([chunkvec(p["ln1_g"][l]), chunkvec(p["ln1_b"][l]), chunkvec(p["conv_b"][l]),
                    chunkvec(p["conv_ln_g"][l]), chunkvec(p["conv_ln_b"][l])], axis=1).astype(np.float32)
    return {"hT": hT_win, "wg": p["ffn1_wg"][l], "wu": p["ffn1_wu"][l], "wd": p["ffn1_wd"][l], "w_in": p["w_in"][l],
            "cwo": p["conv_wo"][l], "convw": np.ascontiguousarray(p["conv_w"][l].T), "lnp": np.ascontiguousarray(lnp),
            "flag": np.full((128, 1), flag, np.float32)}


def c_inputs(yB_win, h1_win, gr_win, cp_win, p, l):
    lnp = np.stack([chunkvec(p["ln2_g"][l]), chunkvec(p["ln2_b"][l]), chunkvec(p["ln3_g"][l]), chunkvec(p["ln3_b"][l])],
                   axis=1).astype(np.float32)
    return {"yB": yB_win, "h1": h1_win, "gr": gr_win, "cp": cp_win, "rwo": p["rwkv_wo"][l], "wout": p["w_out"][l],
            "wg": p["ffn2_wg"][l], "wu": p["ffn2_wu"][l], "wd": p["ffn2_wd"][l], "lnp": np.ascontiguousarray(lnp)}


_PROGS = {}


def _prog(name):
    if name not in _PROGS:
        _PROGS[name] = {"A": build_A, "C": build_C, "B": lambda: build_B(TPB // CH)}[name]()
    return _PROGS[name]


def _windows(full):
    outs = []
    for b in range(NB):
        for j in range(2):
            lo = j * TOK - HALO
            w = np.zeros((full.shape[1], WIN), np.float32)
            s = max(lo, 0)
            w[:, s - lo:] = full[b][:, s:(j + 1) * TOK]
            outs.append(w)
    return outs


def _unwindow(res, key, C):
    full = np.empty((NB, C, T), np.float32)
    for b in range(NB):
        for j in range(2):
            full[b][:, j * TOK:(j + 1) * TOK] = res[2 * b + j][key][:, HALO:]
    return full


def kernel_unfused(**inp):
    p = {k: np.ascontiguousarray(np.asarray(v, dtype=np.float32)) for k, v in inp.items()}
    x = p["x"]
    cores = list(range(8))
    h = np.empty((NB, D, T), np.float32)
    for b in range(NB):
        h[b][:, :NMETA] = p["meta_tokens"].T
        h[b][:, NMETA:] = x[b].T
    for l in range(DEPTH):
        hw = _windows(h)
        in_maps = [a_inputs(hw[c], p, l, float(c % 2)) for c in cores]
        rA = run_bass_kernel_spmd(_prog("A"), in_maps, core_ids=cores).results
        h1 = _unwindow(rA, "h1", D)
        zr = _unwindow(rA, "zr", RWKV_COLS)
        gr = _unwindow(rA, "gr", D)
        cp = _unwindow(rA, "cp", D)
        del rA, in_maps, hw
        in_maps = [b_inputs(zr[c // 2], p, l, c % 2, TPB // CH) for c in cores]
        del zr
        rB = run_bass_kernel_spmd(_prog("B"), in_maps, core_ids=cores).results
        yB = np.empty((NB, D, T), np.float32)
        for c in cores:
            yB[c // 2][(c % 2) * 512:(c % 2 + 1) * 512] = rB[c]["yo"].reshape(512, TPB)[:, :T]
        del rB, in_maps
        yw, h1w, grw, cpw = _windows(yB), _windows(h1), _windows(gr), _windows(cp)
        in_maps = [c_inputs(yw[c], h1w[c], grw[c], cpw[c], p, l) for c in cores]
        rC = run_bass_kernel_spmd(_prog("C"), in_maps, core_ids=cores).results
        h = _unwindow(rC, "hout", D)
        del rC, in_maps
    out = np.empty((NB, SEQ, D), np.float32)
    for b in range(NB):
        out[b] = h[b][:, NMETA:].T
    return out


FBLK = 20
WF = FBLK * NBK
NCHF = TPB // CH

W_SHAPES = [("ffn1_wg", [DEPTH, D, DFF]), ("ffn1_wu", [DEPTH, D, DFF]), ("ffn1_wd", [DEPTH, DFF, D]),
            ("w_in", [DEPTH, D, NIN]), ("conv_wo", [DEPTH, D, D]), ("rwkv_wo", [DEPTH, D, D]), ("w_out", [DEPTH, D, D]),
            ("ffn2_wg", [DEPTH, D, DFF]), ("ffn2_wu", [DEPTH, D, DFF]), ("ffn2_wd", [DEPTH, DFF, D])]


def build_F(depth=DEPTH, fblk=FBLK, nchf=NCHF):
    wf = fblk * NBK
    nc = bass.Bass("TRN2", target_bir_lowering=False)
    g = ExitStack()
    with g:
        S = Sched(nc, g)
        gcx = Ctx(nc, g, S)
        banks = [gcx.ps([128, 512], name="bk%d" % i) for i in range(8)]
        dr = lambda name, shape, kind="ExternalInput": nc.dram_tensor(name, list(shape), F32, kind=kind).ap()
        xT = dr("xT", [D, wf])
        Wd = {k: dr(k, sh) for k, sh in W_SHAPES}
        lnpA = dr("lnpA", [DEPTH, 128, 5, 8]); lnpC = dr("lnpC", [DEPTH, 128, 4, 8]); convw = dr("convw", [DEPTH, D, CW])
        pvB = dr("pvB", [DEPTH, 2, 64, 11, 8]); psmB = dr("psmB", [DEPTH, 128, 4])
        wupB = dr("wupB", [DEPTH, 2, 65, 8, 64]); aupB = dr("aupB", [DEPTH, 2, 65, 8, 64]); gupB = dr("gupB", [DEPTH, 2, 128, 8, 64])
        cst = dr("cst", [64, 6, 64])
        out = dr("out", [D, wf], "ExternalOutput")
        hbuf = dr("hbuf", [D, wf], "Internal"); h1buf = dr("h1buf", [D, wf], "Internal")
        zrbuf = dr("zrbuf", [RWKV_COLS, wf + 1], "Internal")
        grbuf = dr("grbuf", [D, wf], "Internal"); cpbuf = dr("cpbuf", [D, wf], "Internal"); ybuf = dr("ybuf", [D, wf], "Internal")
        with ExitStack() as st:
            cx = Ctx(nc, st, S)
            zt = cx.sb([128, wf - nchf * CH if wf > nchf * CH else 64], name="zt")
            S.op("vector", lambda e: e.memset(zt[0][:], 0.0), writes=[zt[1]])
            for j in range(RWKV_COLS // 128):
                S.op("sync", lambda e, j=j: e.dma_start(out=zrbuf[j * 128:(j + 1) * 128, 0:1], in_=zt[0][:, 0:1], allow_slow_non_contiguous=True),
                     reads=[zt[1]], dma="zi")
            if wf > nchf * CH:
                for c in range(8):
                    S.op("sync", lambda e, c=c: e.dma_start(out=ybuf[c * 128:(c + 1) * 128, nchf * CH:wf], in_=zt[0][:]),
                         reads=[zt[1]], dma="zi")
            S.barrier()
        Wb = {}
        for k, sh in W_SHAPES:
            kc, oc = sh[1] // 128, sh[2] // 128
            Wb[k] = nc.dram_tensor("wb_" + k, [oc, 128, kc, 128], BF16, kind="Internal").ap()
        for l in range(depth):
            for k, sh in W_SHAPES:
                wv = wview(Wd[k][l])
                for oc in range(sh[2] // 128):
                    S.op("gpsimd", lambda e, wb=Wb[k], wv=wv, oc=oc: e.dma_start(out=wb[oc], in_=wv[:, :, oc * 128:(oc + 1) * 128]),
                         dma="cv")
            S.barrier()
            with ExitStack() as st:
                cx = Ctx(nc, st, S)
                tp = TP(cx, banks, NBK, slabmode=True)
                dA = {"hT": xT if l == 0 else hbuf, "wg": Wb["ffn1_wg"], "wu": Wb["ffn1_wu"], "wd": Wb["ffn1_wd"],
                      "w_in": Wb["w_in"], "cwo": Wb["conv_wo"], "convw": convw[l], "lnp": lnpA[l],
                      "h1": h1buf, "zr": zrbuf[:, 1:1 + wf], "gr": grbuf, "cp": cpbuf}
                emit_A(tp, dA, fblk, NBK, use_flag=False)
                S.barrier()
            for hh in range(2):
                with ExitStack() as st:
                    cx = Ctx(nc, st, S)
                    dB = {"rkv_v": zrbuf[0:3 * D, :].rearrange("(w hq c) t -> c w hq t", w=3, hq=NH, c=HS)[:, :, hh * 8:(hh + 1) * 8, :],
                          "zw": zrbuf[3 * D:3 * D + 64, :], "za": zrbuf[3 * D + 64:3 * D + 128, :], "zg": zrbuf[3 * D + 128:3 * D + 256, :],
                          "pv": pvB[l, hh], "psm": psmB[l], "wup": wupB[l, hh], "aup": aupB[l, hh], "gup": gupB[l, hh], "cst": cst,
                          "yo_v": ybuf.rearrange("(hq c) t -> c hq t", c=HS)[:, hh * 8:(hh + 1) * 8, :]}
                    emit_B(cx, banks, dB, nchf)
                    S.barrier()
            with ExitStack() as st:
                cx = Ctx(nc, st, S)
                tp = TP(cx, banks, NBK, slabmode=True)
                dC = {"yB": ybuf, "h1": h1buf, "gr": grbuf, "cp": cpbuf, "rwo": Wb["rwkv_wo"], "wout": Wb["w_out"],
                      "wg": Wb["ffn2_wg"], "wu": Wb["ffn2_wu"], "wd": Wb["ffn2_wd"], "lnp": lnpC[l],
                      "hout": out if l == depth - 1 else hbuf}
                emit_C(tp, dC, fblk, NBK)
                S.barrier()
        S.emit()
    return nc


def f_shared_inputs(p):
    sh = {k: p[k] for k, _ in W_SHAPES}
    sh["lnpA"] = np.ascontiguousarray(np.stack([a_inputs(None, p, l, 0.0)["lnp"] for l in range(DEPTH)]))
    sh["lnpC"] = np.ascontiguousarray(np.stack([np.stack(
        [chunkvec(p["ln2_g"][l]), chunkvec(p["ln2_b"][l]), chunkvec(p["ln3_g"][l]), chunkvec(p["ln3_b"][l])], axis=1)
        for l in range(DEPTH)]).astype(np.float32))
    sh["convw"] = np.ascontiguousarray(np.transpose(p["conv_w"], (0, 2, 1)))
    dummy = np.zeros((RWKV_COLS, 1), np.float32)
    bi = [[b_inputs(dummy, p, l, hh, 1) for hh in range(2)] for l in range(DEPTH)]
    sh["pvB"] = np.ascontiguousarray(np.stack([np.stack([bi[l][hh]["pv"] for hh in range(2)]) for l in range(DEPTH)]))
    sh["psmB"] = np.ascontiguousarray(np.stack([bi[l][0]["psm"] for l in range(DEPTH)]))
    for k, kk in (("wupB", "wup"), ("aupB", "aup"), ("gupB", "gup")):
        sh[k] = np.ascontiguousarray(np.stack([np.stack([bi[l][hh][kk] for hh in range(2)]) for l in range(DEPTH)]))
    sh["cst"] = b_consts()
    return sh


def kernel_fused(**inp):
    p = {k: np.ascontiguousarray(np.asarray(v, dtype=np.float32)) for k, v in inp.items()}
    sh = f_shared_inputs(p)
    in_maps = []
    for b in range(NB):
        xT = np.zeros((D, WF), np.float32)
        xT[:, :NMETA] = p["meta_tokens"].T
        xT[:, NMETA:T] = p["x"][b].T
        m = dict(sh)
        m["xT"] = xT
        in_maps.append(m)
    if "F" not in _PROGS:
        _PROGS["F"] = build_F()
    res = run_bass_kernel_spmd(_PROGS["F"], in_maps, core_ids=list(range(NB))).results
    out = np.empty((NB, SEQ, D), np.float32)
    for b in range(NB):
        out[b] = res[b]["out"][:, NMETA:T].T
    return out


def kernel(**inp):
    return kernel_fused(**inp)
```
